# Optimizing a Trainium2 kernel written in Bass

```python
import jax, jax.numpy as jnp
from jax import lax
import numpy as np

D_MODEL = 1024
BATCH = 8
SEQ = 2048
DEPTH = 2
DEC_BATCH = 128
DEC_SEQ = 4
PAST_LEN = 16384
PAGE_SIZE = 128

N_MIXERS = 2
N_A_LAYERS = (DEPTH + 1) // 2
N_B_LAYERS = DEPTH // 2
EXPAND = 2
D_BRANCH = EXPAND * D_MODEL
CHUNK = 128
N_GROUPS = 8
GROUP_DIM = D_BRANCH // N_GROUPS
CONV_W = 31
EPS = 1e-6

kernel_name = "hybrid_chunkmlp_conformer_step"


def rmsnorm(x, g):
    xf = x.astype(jnp.float32)
    y = xf * lax.rsqrt(jnp.mean(xf * xf, axis=-1, keepdims=True) + EPS)
    return (y * g.astype(jnp.float32)).astype(x.dtype)


def layernorm(x, g, b):
    xf = x.astype(jnp.float32)
    mu = jnp.mean(xf, axis=-1, keepdims=True)
    xc = xf - mu
    var = jnp.mean(xc * xc, axis=-1, keepdims=True)
    y = xc * lax.rsqrt(var + EPS)
    return (y * g.astype(jnp.float32) + b.astype(jnp.float32)).astype(x.dtype)


def spatial_gate(v_chunks, w_s, b_s):
    L = v_chunks.shape[2]
    mask = jnp.tril(jnp.ones((L, L), dtype=w_s.dtype))
    w = w_s[:, :L, :L] * mask
    bias = b_s[:, :L].T
    mixed = jnp.einsum('gts,bcsgd->bctgd', w, v_chunks)
    return mixed + bias[None, None, :, :, None]


def chunk_mlp_branch(h, w_in, ln_g, ln_b, w_s, b_s, w_out):
    B, T, _ = h.shape
    proj = jnp.einsum('btd,de->bte', h, w_in)
    u, v, z = jnp.split(proj, 3, axis=-1)
    u = jax.nn.gelu(u, approximate=False)
    v = layernorm(jax.nn.gelu(v, approximate=False), ln_g, ln_b)
    L = min(T, CHUNK)
    v_chunks = v.reshape(B, T // L, L, N_GROUPS, GROUP_DIM)
    mixed = spatial_gate(v_chunks, w_s, b_s).reshape(B, T, D_BRANCH)
    out = u * mixed * jax.nn.silu(z)
    return jnp.einsum('bte,ed->btd', out, w_out), v


def conv_branch(h, left, w_in, conv_w, conv_b, ln_g, ln_b, w_out):
    proj = jnp.einsum('btd,de->bte', h, w_in)
    a, gl, z = jnp.split(proj, 3, axis=-1)
    g = a * jax.nn.sigmoid(gl)
    xp = jnp.concatenate([left.astype(g.dtype), g], axis=1)
    c = lax.conv_general_dilated(
        xp, conv_w[:, None, :].astype(xp.dtype), window_strides=(1,), padding='VALID',
        dimension_numbers=('NWC', 'WIO', 'NWC'), feature_group_count=D_BRANCH)
    c = c + conv_b
    c = jax.nn.silu(layernorm(c, ln_g, ln_b))
    out = c * jax.nn.silu(z)
    y = jnp.einsum('bte,ed->btd', out, w_out)
    return y, xp[:, -(CONV_W - 1):]


def setup_inputs(seed: int = 0) -> dict:
    key = jax.random.key(seed)
    ks = jax.random.split(key, 20)
    f32 = jnp.float32
    D, E = D_MODEL, D_BRANCH
    nrm = lambda k, s, sc: (jax.random.normal(k, s, f32) * sc)
    return {
        "x_prompt": nrm(ks[0], (BATCH, SEQ, D), 1.0),
        "x_sample": nrm(ks[1], (DEC_BATCH, DEC_SEQ, D), 1.0),
        "state_conv": nrm(ks[2], (N_B_LAYERS, DEC_BATCH, CONV_W - 1, E), 0.5),
        "pre_norm_g": 1.0 + nrm(ks[3], (DEPTH, D), 0.05),
        "post_norm_g": 1.0 + nrm(ks[4], (DEPTH, D), 0.05),
        "a_w_in": nrm(ks[5], (N_A_LAYERS, D, 3 * E), D ** -0.5),
        "a_ln_g": 1.0 + nrm(ks[6], (N_A_LAYERS, E), 0.05),
        "a_ln_b": nrm(ks[7], (N_A_LAYERS, E), 0.02),
        "a_w_s": nrm(ks[8], (N_A_LAYERS, N_GROUPS, CHUNK, CHUNK), CHUNK ** -0.5),
        "a_b_s": 1.0 + nrm(ks[9], (N_A_LAYERS, N_GROUPS, CHUNK), 0.1),
        "a_w_out": nrm(ks[10], (N_A_LAYERS, E, D), E ** -0.5),
        "b_w_in": nrm(ks[11], (N_B_LAYERS, D, 3 * E), D ** -0.5),
        "b_conv_w": nrm(ks[12], (N_B_LAYERS, CONV_W, E), CONV_W ** -0.5),
        "b_conv_b": nrm(ks[13], (N_B_LAYERS, E), 0.02),
        "b_ln_g": 1.0 + nrm(ks[14], (N_B_LAYERS, E), 0.05),
        "b_ln_b": nrm(ks[15], (N_B_LAYERS, E), 0.02),
        "b_w_out": nrm(ks[16], (N_B_LAYERS, E, D), E ** -0.5),
    }


def reference(x_prompt, x_sample, state_conv, pre_norm_g, post_norm_g,
              a_w_in, a_ln_g, a_ln_b, a_w_s, a_b_s, a_w_out,
              b_w_in, b_conv_w, b_conv_b, b_ln_g, b_ln_b, b_w_out):
    xp, xs = x_prompt, x_sample
    conv_prompt_list, conv_sample_list, v_sample_list = [], [], []
    for i in range(DEPTH):
        j = i // N_MIXERS
        hp = rmsnorm(xp, pre_norm_g[i])
        hs = rmsnorm(xs, pre_norm_g[i])
        if i % N_MIXERS == 0:
            yp, _ = chunk_mlp_branch(hp, a_w_in[j], a_ln_g[j], a_ln_b[j], a_w_s[j], a_b_s[j], a_w_out[j])
            ys, vs = chunk_mlp_branch(hs, a_w_in[j], a_ln_g[j], a_ln_b[j], a_w_s[j], a_b_s[j], a_w_out[j])
            v_sample_list.append(vs)
        else:
            left_p = jnp.zeros((xp.shape[0], CONV_W - 1, D_BRANCH), dtype=xp.dtype)
            yp, cp = conv_branch(hp, left_p, b_w_in[j], b_conv_w[j], b_conv_b[j], b_ln_g[j], b_ln_b[j], b_w_out[j])
            ys, cs = conv_branch(hs, state_conv[j], b_w_in[j], b_conv_w[j], b_conv_b[j], b_ln_g[j], b_ln_b[j], b_w_out[j])
            conv_prompt_list.append(cp)
            conv_sample_list.append(cs)
        xp = xp + rmsnorm(yp, post_norm_g[i])
        xs = xs + rmsnorm(ys, post_norm_g[i])
    conv_prompt_new = jnp.stack(conv_prompt_list, axis=0)
    conv_sample_new = jnp.stack(conv_sample_list, axis=0)
    chunk_v_sample = jnp.stack(v_sample_list, axis=0)
    return (xp, xs, conv_prompt_new, conv_sample_new, chunk_v_sample)
```

```python
from contextlib import ExitStack

import numpy as np
import concourse.bass as bass
import concourse.mybir as mybir
from concourse.bass_utils import run_bass_kernel_spmd

F32 = mybir.dt.float32
BF16 = mybir.dt.bfloat16
AF = mybir.ActivationFunctionType
ALU = mybir.AluOpType

N_CORES = 8
D = 1024
E = 2048
SEQ = 2048
NBLK = 4
TB = 512
NS = 64
CW = 31
EPS = 1e-6
RING = 8
UNITS_PER_BLOCK = 32
W_INFLIGHT = 8
NT_DVE = 5
NT_DVE_LAST = 2

COMPUTE = ("pe", "act", "dve", "pool")


class _Op:
    __slots__ = ("idx", "eng", "fn", "deps", "kind", "cnt", "sem", "semval", "waits", "prewait")

    def __init__(self, idx, eng, fn, kind):
        self.idx = idx
        self.eng = eng
        self.fn = fn
        self.kind = kind
        self.deps = []
        self.cnt = 0
        self.sem = None
        self.semval = 0
        self.waits = {}
        self.prewait = None


class Prog:
    def __init__(self, nc, n_dma_sems=8):
        self.nc = nc
        self.ops = []
        self.last_writer = {}
        self.readers = {}
        self.n_dma_sems = n_dma_sems
        self.sems = {}
        self.dma_sems = {}

    def _add(self, eng, fn, kind, reads, writes):
        o = _Op(len(self.ops), eng, fn, kind)
        deps = set()
        for r in reads:
            w = self.last_writer.get(r)
            if w is not None:
                deps.add(w)
        for w_ in writes:
            w = self.last_writer.get(w_)
            if w is not None:
                deps.add(w)
            for rd in self.readers.get(w_, ()):
                deps.add(rd)
        o.deps = list(deps)
        for r in reads:
            self.readers.setdefault(r, []).append(o)
        for w_ in writes:
            self.last_writer[w_] = o
            self.readers[w_] = []
        self.ops.append(o)
        return o

    def c(self, eng, fn, reads=(), writes=()):
        return self._add(eng, fn, "c", reads, writes)

    def dma(self, q, fn, reads=(), writes=(), semkey=None):
        o = self._add(q, fn, "d", reads, writes)
        o.sem = semkey
        return o

    def finalize(self, sem_alloc):
        for e in COMPUTE:
            self.sems[e] = sem_alloc("sem_" + e)
        cnt = {e: 0 for e in COMPUTE}
        rr = {}
        use = {}
        last = {}
        for o in self.ops:
            if o.kind == "c":
                cnt[o.eng] += 1
                o.cnt = cnt[o.eng]
            else:
                key = o.sem
                if key is None:
                    r = rr.get(o.eng, 0)
                    rr[o.eng] = r + 1
                    key = (o.eng, r % self.n_dma_sems)
                if key not in self.dma_sems:
                    self.dma_sems[key] = sem_alloc("dsem_%s_%s" % (key[0], key[1]))
                    use[key] = 0
                use[key] += 1
                o.sem = key
                o.semval = 16 * use[key]
                o.prewait = last.get(key)
                last[key] = o
        for o in self.ops:
            need = {}
            deps = list(o.deps)
            if o.kind == "d" and o.prewait is not None:
                deps.append(o.prewait)
            for d in deps:
                if d.kind == "c":
                    if d.eng == o.eng and o.kind == "c" and d.eng == "pe":
                        continue
                    k = ("c", d.eng)
                    need[k] = max(need.get(k, 0), d.cnt)
                else:
                    k = ("d", d.sem)
                    need[k] = max(need.get(k, 0), d.semval)
            o.waits = need
        seen = {}
        for o in self.ops:
            s = seen.setdefault(o.eng, {})
            w2 = {}
            for k, v in o.waits.items():
                if s.get(k, 0) >= v:
                    continue
                w2[k] = v
                s[k] = v
            o.waits = w2

    def emit(self, block):
        by_eng = {}
        for o in self.ops:
            by_eng.setdefault(o.eng, []).append(o)

        def run(name, eng):
            for o in by_eng.get(name, []):
                for k, v in o.waits.items():
                    if k[0] == "c":
                        eng.wait_ge(self.sems[k[1]], v)
                    else:
                        eng.wait_ge(self.dma_sems[k[1]], v)
                ins = o.fn(eng)
                if o.kind == "c":
                    ins.then_inc(self.sems[o.eng], 1)
                else:
                    ins.then_inc(self.dma_sems[o.sem], 16)
            fin = {}
            for o in by_eng.get(name, []):
                if o.kind == "d":
                    fin[o.sem] = max(fin.get(o.sem, 0), o.semval)
            for k, v in fin.items():
                eng.wait_ge(self.dma_sems[k], v)

        @block.tensor
        def _(e):
            run("pe", e)

        @block.scalar
        def _(e):
            run("act", e)

        @block.vector
        def _(e):
            run("dve", e)

        @block.gpsimd
        def _(e):
            run("pool", e)

        @block.sync
        def _(e):
            run("sp", e)


class Rot:
    def __init__(self, items):
        self.items = list(items)
        self.i = 0

    def next(self):
        v = self.items[self.i % len(self.items)]
        self.i += 1
        return v


def build_nc(debug=False):
    nc = bass.Bass("TRN2", target_bir_lowering=False)

    def din(name, shape):
        return nc.dram_tensor(name, list(shape), F32, kind="ExternalInput").ap()

    def dout(name, shape):
        return nc.dram_tensor(name, list(shape), F32, kind="ExternalOutput").ap()

    xp_d = din("xp", [SEQ, D])
    xs_d = din("xs", [NS, D])
    stT_d = din("stT", [128, 16, 480])
    st_d = din("st", [16, 30, E])
    wst_d = din("wst", [UNITS_PER_BLOCK, 128, 4096])
    gvec_d = din("gvec", [4, D])
    lnA_d = din("lnA", [2, E])
    wsT_d = din("wsT", [128, 8 * 128])
    wsTs_d = din("wsTs", [64, 8 * 64])
    bsA_d = din("bsA", [1, 8 * 128])
    bsS_d = din("bsS", [1, 8 * 64])
    cwT_d = din("cwT", [128, 16 * CW])
    pB_d = din("pB", [128, 48])

    yp_d = dout("yp", [SEQ, D])
    ys_d = dout("ys", [NS, D])
    cp_d = dout("cp", [30, E])
    cs_d = dout("cs", [16, 30, E])
    cv_d = dout("cv", [NS, E])
    if debug:
        dbg_x1 = dout("dbg_x1", [SEQ + NS, D])
        def dout16(name, shape):
            return nc.dram_tensor(name, list(shape), BF16, kind="ExternalOutput").ap()
        dbg_outT = dout16("dbg_outT", [NBLK, 16, 128, 576])
        dbg_hT = dout16("dbg_hT", [2 * NBLK, 128, 8, 576])
        dbg_vn = dout16("dbg_vn", [NBLK, 5, 128, E])
        dbg_GT = dout16("dbg_GT", [NBLK, 16, 128, 544])
        dbg_cT = dout16("dbg_cT", [NBLK, 16, 128, 576])
        dbg_oB = dout16("dbg_oB", [NBLK, 16, 128, 576])
        dbg_q = dout("dbg_q", [NBLK, 16, 128, 576])
        dbg_gu = dout("dbg_gu", [NBLK, 16, 128, 576])

    es = ExitStack()
    with es:
        total_f32 = nc.sbuf_bytes_remaining // 4 - 16
        S = es.enter_context(nc.sbuf_tensor("S", [128, total_f32], F32))
        PS = es.enter_context(nc.psum_tensor("PS", [128, 4096], F32))
        P = Prog(nc)

        cur = [0]

        def f32v(n):
            a = S[:, cur[0]:cur[0] + n]
            cur[0] += n
            return a

        def b16v(n):
            n2 = (n + 1) // 2
            a = S[:, cur[0]:cur[0] + n2].bitcast(BF16)
            cur[0] += n2
            return a

        xres = [f32v(D) for _ in range(5)]
        hT = b16v(8 * 576).rearrange("p (k t) -> p k t", t=576)
        hb = [b16v(D) for _ in range(2)]
        junk = b16v(D)
        ytmpb = [f32v(D) for _ in range(2)]
        gb_pre = f32v(D)
        gb_post = f32v(D)
        ident_b = b16v(128)
        ident_f = f32v(128)
        ones_b = b16v(128)
        WsT = b16v(8 * 128).rearrange("p (g t) -> p g t", t=128)
        WsTs = b16v(8 * 64).rearrange("p (g t) -> p g t", t=64)
        bias2 = b16v(8 * 128)
        bias2s = b16v(8 * 64)
        cw_b = b16v(16 * 32).rearrange("p (i k) -> p i k", k=32)
        halo = b16v(16 * 32).rearrange("p (i k) -> p i k", k=32)
        pB = f32v(48)
        epsc = f32v(1)
        sel0 = f32v(1)
        small = f32v(64)
        dummy = f32v(2)
        ring = [b16v(4096) for _ in range(RING)]
        cacc = [f32v(512) for _ in range(2)] if NT_DVE > 0 else None
        arena0 = cur[0]

        gv = [f32v(E) for _ in range(2)]
        vn = [b16v(E) for _ in range(5)]
        lnbc_lo = cur[0]
        lng_bc = f32v(E)
        lnb_bc = f32v(E)
        lnbc_hi = cur[0]
        outT = [b16v(576) for _ in range(16)]
        gu = [f32v(576) for _ in range(2)]
        szA = [f32v(576) for _ in range(2)]
        bnst = [f32v(24) for _ in range(2)]
        endA = cur[0]

        cur[0] = arena0
        GT = [b16v(544) for _ in range(16)]
        cT = [b16v(576) for _ in range(16)]
        p1_lo = cur[0]
        dg = b16v(32 * 128).rearrange("p (k j) -> p k j", j=128)
        GTs = [b16v(544).rearrange("p (b t) -> p b t", t=34) for _ in range(2)]
        stg = [f32v(480) for _ in range(2)]
        sig = [f32v(576) for _ in range(2)]
        p1_hi = cur[0]
        assert p1_lo <= lnbc_lo and lnbc_hi <= p1_hi, (p1_lo, lnbc_lo, lnbc_hi, p1_hi)
        csq = [b16v(576) for _ in range(2)]
        mu_bc = f32v(576)
        rstd_bc = f32v(576)
        var_t = f32v(576)
        cst_lo = cur[0]
        szT = [f32v(576) for _ in range(2)]
        cn = [f32v(576) for _ in range(2)]
        cstage = S[:, cst_lo:cst_lo + E]
        g32 = [f32v(32) for _ in range(16)]
        g32s = [f32v(64) for _ in range(16)]
        sacc = f32v(128)
        endB = cur[0]
        assert max(endA, endB) <= total_f32, (endA, endB, total_f32)

        arenaA_res = (["gv0", "gv1", "lnbc", "bnst0", "bnst1", "gu0", "gu1", "szA0", "szA1"]
                      + ["vn%d" % j for j in range(5)] + ["outT%d" % i for i in range(16)])
        arenaB_res = (["GT%d" % i for i in range(16)] + ["cT%d" % i for i in range(16)]
                      + ["dgA", "dgB", "GTs0", "GTs1", "stg0", "stg1", "sig0", "sig1", "csq0", "csq1",
                         "mu", "rstd", "var", "szT0", "szT1", "cn0", "cn1"]
                      + ["g32_%d" % i for i in range(16)] + ["sacc"])

        CST = ["szT0", "szT1", "cn0", "cn1"]

        def fence():
            P.c("dve", lambda e: e.memset(dummy[:, 0:1], 0.0), writes=arenaA_res + arenaB_res + ["dummy"])

        def bank(b):
            return PS[:, 512 * b:512 * (b + 1)]

        def slot(k):
            return PS[:, 512 * 3 + 64 * k:512 * 3 + 64 * (k + 1)]

        rotP = Rot([0, 1, 2])
        rotP4 = Rot([0, 1, 2, 3])
        rotM = Rot([4, 5])
        rotY = Rot([(6, 7), (4, 5)])
        srotP = Rot([0, 1, 2, 3])
        srotM = Rot([4, 5])

        wstate = {"issued": 0, "released": 0}
        total_units = NBLK * UNITS_PER_BLOCK

        def w_issue():
            while wstate["issued"] < total_units and wstate["issued"] - RING < wstate["released"]:
                n = wstate["issued"]
                sl = n % RING
                u = n % UNITS_PER_BLOCK
                P.dma("pool", (lambda sl, u: lambda e: e.dma_start(out=ring[sl], in_=wst_d[u]))(sl, u),
                      writes=["w%d" % sl, "wthr%d" % (n % W_INFLIGHT)], semkey=("w", sl))
                wstate["issued"] += 1

        def w_release(k):
            wstate["released"] += k
            w_issue()

        def wslot(blk, u):
            n = blk * UNITS_PER_BLOCK + u
            assert n < wstate["issued"], (n, wstate)
            return ring[n % RING], "w%d" % (n % RING)

        P.c("pool", lambda e: e.memset(ident_f, 1.0), writes=["ident_f"])
        P.c("pool", lambda e: e.affine_select(out=ident_f, in_=ident_f, pattern=[[-1, 128]],
                                               compare_op=ALU.is_equal, fill=0.0, base=0,
                                               channel_multiplier=1), reads=["ident_f"], writes=["ident_f"])
        P.c("pool", lambda e: e.tensor_copy(out=ident_b, in_=ident_f), reads=["ident_f"], writes=["ident_b"])
        P.c("pool", lambda e: e.memset(ones_b, 1.0), writes=["ones_b"])
        P.c("pool", lambda e: e.memset(epsc, EPS), writes=["epsc"])
        P.c("pool", lambda e: e.memset(halo, 0.0), writes=["halo%d" % i for i in range(16)])
        P.c("pool", lambda e: e.memset(sel0, 1.0), writes=["sel0"])
        P.c("pool", lambda e: e.affine_select(out=sel0, in_=sel0, pattern=[[0, 1]], compare_op=ALU.is_equal,
                                               fill=0.0, base=0, channel_multiplier=1),
            reads=["sel0"], writes=["sel0"])
        w_issue()

        def load_post(layer):
            P.dma("sp", lambda e: e.dma_start(out=gb_post, in_=gvec_d[2 + layer:3 + layer, :].broadcast_to([128, D])),
                  writes=["gb_post"])

        def load_pre(nl):
            P.dma("sp", lambda e: e.dma_start(out=gb_pre, in_=gvec_d[nl:nl + 1, :].broadcast_to([128, D])),
                  writes=["gb_pre"])
        P.dma("sp", lambda e: e.dma_start(out=gb_pre, in_=gvec_d[0:1, :].broadcast_to([128, D])), writes=["gb_pre"])
        P.dma("sp", lambda e: e.dma_start(out=pB, in_=pB_d), writes=["pB"])
        cur[0] = arena0
        st_cw = f32v(16 * CW)
        st_ws = f32v(1024)
        st_wss_full = f32v(512)
        setup_tmp = ["st_cw", "st_ws", "st_wss"]
        P.dma("sp", lambda e: e.dma_start(out=st_cw, in_=cwT_d), writes=["st_cw"])
        P.c("dve", lambda e: e.memset(cw_b, 0.0), writes=["cw_b"])
        P.c("dve", lambda e: e.tensor_copy(out=cw_b[:, :, 0:CW], in_=st_cw.rearrange("p (i k) -> p i k", k=CW)),
            reads=["st_cw"], writes=["cw_b"])
        P.dma("sp", lambda e: e.dma_start(out=st_ws, in_=wsT_d), writes=["st_ws"])
        ws3 = st_ws.rearrange("p (g t) -> p g t", t=128)
        P.c("pool", lambda e: e.affine_select(out=ws3, in_=ws3, pattern=[[0, 8], [1, 128]], compare_op=ALU.is_ge,
                                               fill=0.0, base=0, channel_multiplier=-1),
            reads=["st_ws"], writes=["st_ws"])
        P.c("pool", lambda e: e.tensor_copy(out=WsT, in_=ws3), reads=["st_ws"], writes=["WsT"])
        st_wss = st_wss_full[0:64, :]
        P.dma("sp", lambda e: e.dma_start(out=st_wss, in_=wsTs_d), writes=["st_wss"])
        wss4 = st_wss.rearrange("p (g b t) -> p g b t", b=16, t=4)
        P.c("pool", lambda e: e.affine_select(out=wss4, in_=wss4, pattern=[[0, 8], [4, 16], [1, 4]],
                                               compare_op=ALU.is_ge, fill=0.0, base=0, channel_multiplier=-1),
            reads=["st_wss"], writes=["st_wss"])
        P.c("pool", lambda e: e.affine_select(out=wss4, in_=wss4, pattern=[[0, 8], [-4, 16], [0, 4]],
                                               compare_op=ALU.is_ge, fill=0.0, base=0, channel_multiplier=1),
            reads=["st_wss"], writes=["st_wss"])
        P.c("pool", lambda e: e.tensor_copy(out=WsTs[0:64], in_=st_wss.rearrange("p (g t) -> p g t", t=64)),
            reads=["st_wss"], writes=["WsTs"])

        def bias_rows(src_d, n, dst, tag):
            t32 = f32v(n)[0:2, :]
            thi = b16v(n)[0:2, :]
            tlo = b16v(n)[0:2, :]
            td = f32v(n)[0:2, :]
            r = "bias_" + tag
            setup_tmp.extend([r, r + "h", r + "l", r + "d"])
            P.dma("sp", lambda e: e.dma_start(out=t32, in_=src_d.broadcast_to([2, n])), writes=[r])
            P.c("dve", lambda e: e.tensor_copy(out=thi, in_=t32), reads=[r], writes=[r + "h"])
            P.c("dve", lambda e: e.tensor_tensor(out=td, in0=t32, in1=thi, op=ALU.subtract),
                reads=[r, r + "h"], writes=[r + "d"])
            P.c("dve", lambda e: e.tensor_copy(out=tlo, in_=td), reads=[r + "d"], writes=[r + "l"])
            P.c("dve", lambda e: e.tensor_tensor(out=td, in0=thi, in1=tlo, op=ALU.subtract),
                reads=[r + "h", r + "l", r + "d"], writes=[r + "d"])
            P.c("dve", lambda e: e.scalar_tensor_tensor(out=dst[0:2, :], in0=td, scalar=sel0[0:2, :], in1=tlo,
                                                         op0=ALU.mult, op1=ALU.add),
                reads=[r + "d", r + "l", "sel0"], writes=[tag])

        bias_rows(bsA_d, 1024, bias2, "bias2")
        bias_rows(bsS_d, 512, bias2s, "bias2s")
        assert cur[0] <= total_f32
        P.c("dve", lambda e: e.memset(dummy[:, 1:2], 0.0), writes=setup_tmp + arenaA_res + arenaB_res + ["dummy2"])

        P.dma("sp", lambda e: e.dma_start(out=cs_d[:, 0:26, :], in_=st_d[:, 4:30, :]))

        def tiles_of(blk):
            t = [(j, 128, j * 128) for j in range(4)]
            if blk == NBLK - 1:
                t.append((4, NS, 512))
            return t

        def segs_of(blk):
            s = [(0, 512)]
            if blk == NBLK - 1:
                s.append((512, NS))
            return s

        def hT_res(c0, n):
            return ["hT%d" % j for j in range(c0 // 128, (c0 + n + 127) // 128)]

        SD_COL = {"pro": 8, "epi": 9}

        def rsqrt_chain(ss, nr, scale, rs_out, tag):
            sd = small[:, SD_COL[tag]:SD_COL[tag] + 1]
            P.c("act", lambda e: e.activation(out=sd[:nr], in_=ss[:nr], func=AF.Sqrt, bias=epsc[:nr], scale=scale),
                reads=[tag + "_ss", "epsc"], writes=[tag + "_sd"])
            P.c("dve", lambda e: e.reciprocal(out=rs_out[:nr], in_=sd[:nr]), reads=[tag + "_sd"], writes=[tag + "_rs"])

        def prologue_dma(layer, blk, tl):
            j, nr, c0 = tl
            xr = "xres%d" % j
            if layer == 0:
                src = xs_d if j == 4 else xp_d[blk * TB + j * 128: blk * TB + (j + 1) * 128, :]
                P.dma("sp", lambda e: e.dma_start(out=xres[j][:nr], in_=src), writes=[xr])

        def prologue_pre(layer, blk, tl):
            j, nr, c0 = tl
            xr = "xres%d" % j
            ss = small[:, 0:1]
            rs = small[:, 1:2]
            hbuf = hb[j % 2]
            hres = "hb%d" % (j % 2)
            P.c("act", lambda e: e.activation(out=junk[:nr], in_=xres[j][:nr], func=AF.Square, accum_out=ss[:nr]),
                reads=[xr], writes=["junk", "pro_ss"])
            rsqrt_chain(ss, nr, 1.0 / D, rs, "pro")
            P.c("dve", lambda e: e.scalar_tensor_tensor(out=hbuf[:nr], in0=xres[j][:nr], scalar=rs[:nr],
                                                         in1=gb_pre[:nr], op0=ALU.mult, op1=ALU.mult),
                reads=[xr, "pro_rs", "gb_pre"], writes=[hres])

        def prologue_T(layer, blk, tl):
            j, nr, c0 = tl
            hbuf = hb[j % 2]
            hres = "hb%d" % (j % 2)
            b = rotP.next()
            pb = bank(b).bitcast(BF16).rearrange("p (k t) -> p k t", t=128)

            def tr(e):
                ins = None
                for k in range(8):
                    ins = e.transpose(pb[:, k, 0:nr], hbuf[:nr, k * 128:(k + 1) * 128], ident_b[:nr, :nr])
                return ins
            P.c("pe", tr, reads=[hres, "ident_b"], writes=["pb%d" % b])
            P.c("act", lambda e: e.activation(out=hT[:, :, c0:c0 + nr], in_=pb[:, :, 0:nr], func=AF.Copy),
                reads=["pb%d" % b], writes=["hT%d" % j])

        def prologue(layer, blk, tl):
            prologue_dma(layer, blk, tl)
            prologue_pre(layer, blk, tl)
            prologue_T(layer, blk, tl)

        def proj_fm(blk, wt, wres, sub, segs, tag, rot=None):
            outs = []
            for (c0, n) in segs:
                b = (rot or rotP).next()
                pa, pr = bank(b)[:, 0:n], "pb%d" % b

                def mm(e, pa=pa, c0=c0, n=n):
                    ins = None
                    for k in range(8):
                        ins = e.matmul(pa, lhsT=wt[:, sub * 1024 + k * 128: sub * 1024 + (k + 1) * 128],
                                       rhs=hT[:, k, c0:c0 + n], start=(k == 0), stop=(k == 7))
                    return ins
                P.c("pe", mm, reads=[wres] + hT_res(c0, n), writes=[pr])
                outs.append((pa, pr, c0, n))
            return outs

        def o_phase(layer, blk, outbuf, outres_fn, next_pro, defer_T=False):
            tls = tiles_of(blk)
            ubase = 12 if layer == 0 else 28
            def o_tile(tl, k, tgts):
                j, nr, c0 = tl
                bks = rotY.next()

                yres = ["pb%d" % bks[0], "pb%d" % bks[1]]
                for g4 in range(4):
                    def mm(e, bks=bks, nr=nr, c0=c0, g4=g4):
                        ins = None
                        wt, _ = wslot(blk, ubase + g4)
                        for kt in range(4 * g4, 4 * g4 + 4):
                            for h in range(2):
                                ins = e.matmul(bank(bks[h])[:nr, :], lhsT=outbuf[kt][:, c0:c0 + nr],
                                               rhs=wt[:, (kt % 4) * 1024 + h * 512:(kt % 4) * 1024 + (h + 1) * 512],
                                               start=(kt == 0), stop=(kt == 15))
                        return ins
                    P.c("pe", mm, reads=[wslot(blk, ubase + g4)[1]] + [outres_fn(i) for i in range(4 * g4, 4 * g4 + 4)],
                        writes=yres)
                    if k == len(tls) - 1:
                        w_release(1)
                if k >= 2 and tgts[k - 2] is not None:
                    prologue_T(*tgts[k - 2])
                rs = small[:, 4:5]
                sst = small[:, 5:6]
                assert bks[1] == bks[0] + 1
                y2 = PS[:, 512 * bks[0]:512 * bks[0] + 1024]
                P.c("act", lambda e: e.activation(out=junk[:nr, :], in_=y2[:nr, :], func=AF.Square, accum_out=sst[:nr]),
                    reads=yres, writes=["junk", "epi_ss"])
                rsqrt_chain(sst, nr, 1.0 / D, rs, "epi")
                ytmp = ytmpb[j % 2]
                yt = "ytmp%d" % (j % 2)
                P.c("dve", lambda e: e.scalar_tensor_tensor(out=ytmp[:nr, :], in0=y2[:nr, :], scalar=rs[:nr],
                                                            in1=gb_post[:nr, :], op0=ALU.mult, op1=ALU.mult),
                    reads=yres + ["epi_rs", "gb_post"], writes=[yt + "_0", yt + "_1", yt])
                xr = "xres%d" % j
                if layer == 0:
                    P.c("dve", lambda e: e.tensor_tensor(out=xres[j][:nr], in0=xres[j][:nr], in1=ytmp[:nr], op=ALU.add),
                        reads=[xr, yt + "_0", yt + "_1"], writes=[xr])
                else:
                    P.c("dve", lambda e: e.tensor_tensor(out=ytmp[:nr], in0=xres[j][:nr], in1=ytmp[:nr], op=ALU.add),
                        reads=[xr, yt + "_0", yt + "_1"], writes=[yt + "_0", yt + "_1", yt])
                if layer == 0 and debug:
                    ddst = dbg_x1[SEQ:SEQ + NS, :] if j == 4 else dbg_x1[blk * TB + j * 128: blk * TB + (j + 1) * 128, :]
                    P.dma("sp", lambda e: e.dma_start(out=ddst, in_=xres[j][:nr]), reads=[xr])
                if layer == 1:
                    dst = ys_d if j == 4 else yp_d[blk * TB + j * 128: blk * TB + (j + 1) * 128, :]
                    P.dma("sp", lambda e: e.dma_start(out=dst, in_=ytmp[:nr]), reads=[yt])
                tgt = next_pro(tl)
                if tgt is not None:
                    prologue_dma(*tgt)
                return tgt
            tgts = []
            n = len(tls)
            for k, tl in enumerate(tls):
                tgts.append(o_tile(tl, k, tgts))
                if k >= 1 and tgts[k - 1] is not None and not (defer_T and k - 1 >= n - 2):
                    prologue_pre(*tgts[k - 1])
            pend = [tgts[k] for k in (n - 2, n - 1) if k >= 0 and tgts[k] is not None]
            if defer_T:
                return pend
            if tgts[n - 1] is not None:
                prologue_pre(*tgts[n - 1])
            for tgt in pend:
                prologue_T(*tgt)
            return []

        def load_state(i, extra):
            par = i % 2
            P.dma("sp", lambda e: e.dma_start(out=stg[par], in_=stT_d[:, i, :]), writes=["stg%d" % par] + extra)

        def load_lnbc(extra):
            P.dma("sp", lambda e: e.dma_start(out=lng_bc, in_=lnA_d[0:1, :].broadcast_to([128, E])), writes=["lnbc"] + extra)
            P.dma("sp", lambda e: e.dma_start(out=lnb_bc, in_=lnA_d[1:2, :].broadcast_to([128, E])), writes=["lnbc"] + extra)

        def layer_a(blk, next_pro, pendT=()):
            tls = tiles_of(blk)
            segs = segs_of(blk)
            load_post(0)
            ncolA = 512 + (NS if blk == NBLK - 1 else 0)
            if blk == 0:
                load_lnbc([])
            def v_tile(j, nr, c0, after_cb0=None):
                g_ = gv[j % 2]
                gres = "gv%d" % (j % 2)
                st_ = bnst[j % 2]
                sres = "bnst%d" % (j % 2)
                for cb in range(4):
                    wt, wres = wslot(blk, cb)
                    b = rotP.next()

                    def mm(e, b=b, wt=wt, nr=nr, c0=c0):
                        ins = None
                        for k in range(8):
                            ins = e.matmul(bank(b)[:nr, :], lhsT=hT[:, k, c0:c0 + nr],
                                           rhs=wt[:, k * 512:(k + 1) * 512], start=(k == 0), stop=(k == 7))
                        return ins
                    P.c("pe", mm, reads=[wres, "hT%d" % j], writes=["pb%d" % b])
                    P.c("act", (lambda b, cb: lambda e: e.activation(out=g_[:nr, cb * 512:(cb + 1) * 512],
                                                                     in_=bank(b)[:nr, :], func=AF.Gelu))(b, cb),
                        reads=["pb%d" % b], writes=[gres + "_%d" % cb, gres])
                    P.c("dve", (lambda cb: lambda e: e.bn_stats(out=st_[:nr, cb * 6:(cb + 1) * 6],
                                                                in_=g_[:nr, cb * 512:(cb + 1) * 512]))(cb),
                        reads=[gres + "_%d" % cb], writes=[sres + "_%d" % cb])
                    if cb == 0 and after_cb0 is not None:
                        after_cb0()
                mv = small[:, 16 + 8 * (j % 2):18 + 8 * (j % 2)]
                P.c("dve", lambda e: e.bn_aggr(out=mv[:nr], in_=st_[:nr, 0:24]),
                    reads=[sres + "_%d" % cb for cb in range(4)], writes=["ln_mv%d" % (j % 2), sres])

            def v_tail_a(j, nr, c0):
                q = j % 2
                mv = small[:, 16 + 8 * q:18 + 8 * q]
                rs = small[:, 18 + 8 * q:19 + 8 * q]
                nb = small[:, 19 + 8 * q:20 + 8 * q]
                sd = small[:, 20 + 8 * q:21 + 8 * q]
                P.c("act", lambda e: e.activation(out=sd[:nr], in_=mv[:nr, 1:2], func=AF.Sqrt, bias=epsc[:nr], scale=1.0),
                    reads=["ln_mv%d" % q, "epsc"], writes=["ln_sd%d" % q])
                P.c("dve", lambda e: e.reciprocal(out=rs[:nr], in_=sd[:nr]), reads=["ln_sd%d" % q], writes=["ln_rs%d" % q])
                P.c("dve", lambda e: e.scalar_tensor_tensor(out=nb[:nr], in0=mv[:nr, 0:1], scalar=-1.0, in1=rs[:nr],
                                                             op0=ALU.mult, op1=ALU.mult),
                    reads=["ln_mv%d" % q, "ln_rs%d" % q], writes=["ln_nb%d" % q])

            def v_tail(j, nr, c0):
                g_ = gv[j % 2]
                gres = "gv%d" % (j % 2)
                q = j % 2
                rs = small[:, 18 + 8 * q:19 + 8 * q]
                nb = small[:, 19 + 8 * q:20 + 8 * q]
                allg = [gres + "_%d" % cb for cb in range(4)]
                P.c("act", lambda e: e.activation(out=g_[:nr], in_=g_[:nr], func=AF.Identity, scale=rs[:nr], bias=nb[:nr]),
                    reads=allg + ["ln_rs%d" % q, "ln_nb%d" % q], writes=allg + [gres])
                P.c("dve", lambda e: e.tensor_tensor(out=g_[:nr], in0=g_[:nr], in1=lng_bc[:nr], op=ALU.mult),
                    reads=[gres, "lnbc"], writes=[gres])
                if j == 4:
                    P.c("dve", lambda e: e.tensor_tensor(out=g_[:nr], in0=g_[:nr], in1=lnb_bc[:nr], op=ALU.add),
                        reads=[gres, "lnbc"], writes=[gres])
                    P.dma("sp", lambda e: e.dma_start(out=cv_d, in_=g_[:nr]), reads=[gres])
                    P.c("dve", lambda e: e.tensor_copy(out=vn[j][:nr], in_=g_[:nr]), reads=[gres], writes=["vn%d" % j])
                else:
                    P.c("dve", lambda e: e.tensor_tensor(out=vn[j][:nr], in0=g_[:nr], in1=lnb_bc[:nr], op=ALU.add),
                        reads=[gres, "lnbc"], writes=["vn%d" % j])
            pendT = list(pendT)
            pendP = list(pendT)
            for idx, (j, nr, c0) in enumerate(tls):
                while pendT and pendT[0][2][0] <= j:
                    prologue_T(*pendT.pop(0))

                def hook(idx=idx, j=j):
                    if idx >= 1:
                        v_tail_a(*tls[idx - 1])
                    while pendP and pendP[0][2][0] <= j + 1:
                        prologue_pre(*pendP.pop(0))
                v_tile(j, nr, c0, after_cb0=hook)
                if idx >= 1:
                    v_tail(*tls[idx - 1])
            v_tail_a(*tls[-1])
            v_tail(*tls[-1])
            load_pre(1)
            if blk == NBLK - 1:
                load_state(0, ["lnbc"])
                load_state(1, ["lnbc"])
            w_release(4)
            pend = []

            def mix_and_gate(g, items):
                for (dt, us, zs, par) in items:
                    mouts = []
                    for (c0, n) in segs:
                        if n == 512:
                            b = rotM.next()
                            pa, pr = bank(b), "pb%d" % b

                            def mm(e, pa=pa, dt=dt, g=g):
                                rb = bias2[0:2, g * 128:(g + 1) * 128].unsqueeze(1).broadcast_to([2, 4, 128])
                                e.matmul(pa, lhsT=ones_b[0:2, :], rhs=rb, start=True, stop=False)
                                ins = None
                                for j in range(4):
                                    ins = e.matmul(pa[:, j * 128:(j + 1) * 128], lhsT=vn[j][:, dt * 128:(dt + 1) * 128],
                                                   rhs=WsT[:, g, :], start=False, stop=(j == 3))
                                return ins
                            P.c("pe", mm, reads=["bias2", "ones_b", "WsT"] + ["vn%d" % j for j in range(4)], writes=[pr])
                        else:
                            b = rotM.next()
                            pa, pr = bank(b)[:, 0:n], "pb%d" % b

                            def mm(e, pa=pa, dt=dt, g=g):
                                e.matmul(pa, lhsT=ones_b[0:2, :], rhs=bias2s[0:2, g * 64:(g + 1) * 64],
                                         start=True, stop=False)
                                return e.matmul(pa, lhsT=vn[4][0:64, dt * 128:(dt + 1) * 128], rhs=WsTs[0:64, g, :],
                                                start=False, stop=True)
                            P.c("pe", mm, reads=["bias2s", "ones_b", "WsTs", "vn4"], writes=[pr])
                        mouts.append((pa, pr, c0, n))
                    gures, szres = "gu%d" % par, "szA%d" % par
                    P.c("dve", (lambda par: lambda e: e.tensor_tensor(out=gu[par][:, 0:ncolA], in0=gu[par][:, 0:ncolA],
                                                                      in1=szA[par][:, 0:ncolA], op=ALU.mult))(par),
                        reads=[gures, szres], writes=[gures])
                    if debug:
                        P.dma("sp", (lambda dt, par: lambda e: e.dma_start(out=dbg_q[blk, dt, :, 0:ncolA], in_=gu[par][:, 0:ncolA]))(dt, par),
                              reads=[gures])
                    for (pa, pr, c0, n) in mouts:
                        P.c("dve", (lambda pa, c0, n, dt, par: lambda e: e.tensor_tensor(
                            out=outT[dt][:, c0:c0 + n], in0=pa, in1=gu[par][:, c0:c0 + n], op=ALU.mult))(pa, c0, n, dt, par),
                            reads=[pr, gures], writes=["outT%d" % dt])

            for g in range(8):
                wt, wres = wslot(blk, 4 + g)
                items = []
                us_all, zs_all = [], []
                for q in range(2):
                    us_all.append(proj_fm(blk, wt, wres, q, segs, "u", rot=rotP4))
                for q in range(2):
                    dt = 2 * g + q
                    par = dt % 2
                    for (pa, pr, c0, n) in us_all[q]:
                        P.c("act", (lambda pa, c0, n, par: lambda e: e.activation(out=gu[par][:, c0:c0 + n], in_=pa, func=AF.Gelu))(pa, c0, n, par),
                            reads=[pr], writes=["gu%d" % par])
                for q in range(2):
                    zs_all.append(proj_fm(blk, wt, wres, 2 + q, segs, "z", rot=rotP4))
                for q in range(2):
                    dt = 2 * g + q
                    par = dt % 2
                    for (pa, pr, c0, n) in zs_all[q]:
                        P.c("act", (lambda pa, c0, n, par: lambda e: e.activation(out=szA[par][:, c0:c0 + n], in_=pa, func=AF.Silu))(pa, c0, n, par),
                            reads=[pr], writes=["szA%d" % par])
                    items.append((dt, us_all[q], zs_all[q], par))
                    if debug:
                        P.dma("sp", (lambda dt, par: lambda e: e.dma_start(out=dbg_gu[blk, dt, :, 0:ncolA], in_=gu[par][:, 0:ncolA]))(dt, par),
                              reads=["gu%d" % par])
                mix_and_gate(g, items)
                w_release(1)
            if debug:
                nc_ = ncolA
                for i in range(16):
                    P.dma("sp", (lambda i: lambda e: e.dma_start(out=dbg_outT[blk, i, :, 0:nc_], in_=outT[i][:, 0:nc_]))(i),
                          reads=["outT%d" % i])
                for (j, nr, c0) in tls:
                    P.dma("sp", (lambda j, nr: lambda e: e.dma_start(out=dbg_vn[blk, j, 0:nr, :], in_=vn[j][:nr]))(j, nr),
                          reads=["vn%d" % j])
                P.dma("sp", lambda e: e.dma_start(out=dbg_hT[2 * blk, :, :, 0:nc_], in_=hT[:, :, 0:nc_]),
                      reads=["hT%d" % t[0] for t in tls])
            o_phase(0, blk, outT, lambda i: "outT%d" % i, next_pro)

        def layer_b(blk, next_pro):
            segs = segs_of(blk)
            last = (blk == NBLK - 1)
            load_post(1)
            load_pre(0)
            rotB = rotP4
            ncol = 512 + (NS if last else 0)
            stat_b = (6, 7)

            ntv = NT_DVE_LAST if last else NT_DVE
            kpe0 = 0 if last else ntv
            ksplit = 16 if last else (ntv + (32 - ntv) // 2)

            def build_diag(i, half):
                k0, k1 = (kpe0, ksplit) if half == 0 else (ksplit, 32)
                in0 = ident_b.unsqueeze(1).broadcast_to([128, k1 - k0, 128])
                in1 = cw_b[:, i, k0:k1].unsqueeze(2).broadcast_to([128, k1 - k0, 128])
                P.c("dve", lambda e: e.tensor_tensor(out=dg[:, k0:k1, :], in0=in0, in1=in1, op=ALU.mult),
                    reads=["ident_b", "cw_b"], writes=["dgA" if half == 0 else "dgB"])

            def glu_pre(i):
                par = i % 2
                P.c("act", lambda e: e.activation(out=GT[i][:, 0:30], in_=halo[:, i, 0:30], func=AF.Copy),
                    reads=["halo%d" % i], writes=["GT%d" % i])
                if last:
                    P.c("act", lambda e: e.activation(out=GTs[par][:, :, 0:30],
                                                      in_=stg[par].rearrange("p (b t) -> p b t", t=30), func=AF.Copy),
                        reads=["stg%d" % par], writes=["GTs%d" % par])
                    if i + 2 < 16:
                        load_state(i + 2, [])

            def glu(i, a_outs, gl_outs):
                par = i % 2
                for (pa, pr, c0, n) in gl_outs:
                    P.c("act", (lambda pa, c0, n: lambda e: e.activation(out=sig[par][:, c0:c0 + n], in_=pa, func=AF.Sigmoid))(pa, c0, n),
                        reads=[pr], writes=["sig%d" % par])
                for (pa, pr, c0, n) in a_outs:
                    if n == 512:
                        P.c("dve", (lambda pa: lambda e: e.tensor_tensor(out=GT[i][:, 30:542], in0=pa, in1=sig[par][:, 0:512], op=ALU.mult))(pa),
                            reads=[pr, "sig%d" % par], writes=["GT%d" % i])
                        if last:
                            P.c("dve", (lambda pa: lambda e: e.tensor_tensor(out=g32[i][:, 0:30], in0=pa[:, 482:512],
                                                                             in1=sig[par][:, 482:512], op=ALU.mult))(pa),
                                reads=[pr, "sig%d" % par], writes=["g32_%d" % i])
                    else:
                        P.c("dve", (lambda pa: lambda e: e.tensor_tensor(out=g32s[i], in0=pa, in1=sig[par][:, 512:576], op=ALU.mult))(pa),
                            reads=[pr, "sig%d" % par], writes=["g32_%d" % i])
                        P.c("dve", lambda e: e.tensor_copy(out=GTs[par][:, :, 30:34],
                                                           in_=g32s[i].rearrange("p (b t) -> p b t", t=4)),
                            reads=["g32_%d" % i], writes=["GTs%d" % par])

            def conv(i, mid=None):
                par = i % 2
                couts = []
                for (c0, n) in segs:
                    b = rotM.next()
                    pa, pr = bank(b)[:, 0:n], "pb%d" % b
                    couts.append((pa, pr, c0, n))
                for half in range(2):
                    if half == 1 and mid is not None:
                        mid()
                    k0, k1 = (kpe0, ksplit) if half == 0 else (ksplit, CW)
                    for (pa, pr, c0, n) in couts:
                        def mm(e, pa=pa, n=n, k0=k0, k1=k1):
                            ins = None
                            kf = ntv if n == 512 else 0
                            for k in range(max(k0, kf), k1):
                                rhs = GT[i][:, k:k + 512] if n == 512 else GTs[par][:, :, k:k + 4]
                                ins = e.matmul(pa, lhsT=dg[:, k, :], rhs=rhs, start=(k == kf), stop=(k == CW - 1))
                            return ins
                        src = "GT%d" % i if n == 512 else "GTs%d" % par
                        P.c("pe", mm, reads=["dgA" if half == 0 else "dgB", src], writes=[pr])
                if i + 1 < 16:
                    build_diag(i + 1, 0)
                    build_diag(i + 1, 1)
                P.c("act", lambda e: e.activation(out=halo[:, i, 0:30], in_=GT[i][:, 512:542], func=AF.Copy),
                    reads=["GT%d" % i], writes=["halo%d" % i])
                for (pa, pr, c0, n) in couts:
                    if n == 512 and ntv > 0:
                        P.c("dve", (lambda pa: lambda e: e.scalar_tensor_tensor(out=cT[i][:, 0:512], in0=pa, scalar=pB[:, i:i + 1],
                                                                               in1=cacc[par], op0=ALU.add, op1=ALU.add))(pa),
                            reads=[pr, "pB", "cacc%d" % par], writes=["cT%d" % i])
                        P.c("act", lambda e: e.activation(out=csq[par][:, 0:512], in_=cT[i][:, 0:512], func=AF.Square),
                            reads=["cT%d" % i], writes=["csq%d" % par])
                    else:
                        P.c("act", (lambda pa, c0, n: lambda e: e.activation(out=cT[i][:, c0:c0 + n], in_=pa, func=AF.Identity,
                                                                             bias=pB[:, i:i + 1], scale=1.0))(pa, c0, n),
                            reads=[pr, "pB"], writes=["cT%d" % i])
                        P.c("act", (lambda pa, c0, n: lambda e: e.activation(out=csq[par][:, c0:c0 + n], in_=pa, func=AF.Square,
                                                                             bias=pB[:, i:i + 1], scale=1.0))(pa, c0, n),
                            reads=[pr, "pB"], writes=["csq%d" % par])

            def conv_taps_dve(i):
                par = i % 2
                if ntv == 0:
                    return
                P.c("dve", lambda e: e.tensor_scalar(out=cacc[par], in0=GT[i][:, 0:512], scalar1=cw_b[:, i, 0:1], scalar2=None,
                                                     op0=ALU.mult),
                    reads=["GT%d" % i, "cw_b"], writes=["cacc%d" % par])
                for k in range(1, ntv):
                    P.c("dve", (lambda k: lambda e: e.scalar_tensor_tensor(out=cacc[par], in0=GT[i][:, k:k + 512],
                                                                           scalar=cw_b[:, i, k:k + 1], in1=cacc[par],
                                                                           op0=ALU.mult, op1=ALU.add))(k),
                        reads=["GT%d" % i, "cw_b", "cacc%d" % par], writes=["cacc%d" % par])

            def stats(i):
                par = i % 2
                for (c0, n) in segs:
                    if n == 512:
                        pS, pQ, rS, rQ = bank(6), bank(7), "pb6", "pb7"
                    else:
                        b = rotP4.next()
                        pS, pQ, rS, rQ = bank(b)[:, 0:64], bank(b)[:, 64:128], "pb%d" % b, "pb%d" % b

                        def mms(e, pS=pS, pQ=pQ, c0=c0, n=n):
                            e.matmul(pS, lhsT=ones_b, rhs=cT[i][:, c0:c0 + n], start=True, stop=False)
                            return e.matmul(pQ, lhsT=ones_b, rhs=csq[par][:, c0:c0 + n], start=False, stop=True)
                        P.c("pe", mms, reads=["ones_b", "cT%d" % i, "csq%d" % par], writes=[rS])
                        if i == 0:
                            P.c("dve", (lambda b: lambda e: e.tensor_copy(out=sacc, in_=bank(b)[:, 0:128]))(b),
                                reads=[rS], writes=["sacc"])
                        else:
                            P.c("dve", (lambda b: lambda e: e.tensor_tensor(out=sacc, in0=sacc, in1=bank(b)[:, 0:128], op=ALU.add))(b),
                                reads=[rS, "sacc"], writes=["sacc"])
                        continue

                    def mm(e, pS=pS, pQ=pQ, c0=c0, n=n):
                        e.matmul(pS, lhsT=ones_b, rhs=cT[i][:, c0:c0 + n], start=(i == 0), stop=(i == 15))
                        return e.matmul(pQ, lhsT=ones_b, rhs=csq[par][:, c0:c0 + n], start=(i == 0), stop=(i == 15))
                    P.c("pe", mm, reads=["ones_b", "cT%d" % i, "csq%d" % par], writes=[rS, rQ])

            build_diag(0, 0)
            build_diag(0, 1)
            for i in range(16):
                wt, wres = wslot(blk, 16 + i // 2)
                sub = (i % 2) * 2
                glu_pre(i)
                for seg in segs:
                    a_outs = proj_fm(blk, wt, wres, sub, [seg], "a", rot=rotB)
                    gl_outs = proj_fm(blk, wt, wres, sub + 1, [seg], "gl", rot=rotB)
                    glu(i, a_outs, gl_outs)
                if i >= 1:
                    conv(i - 1, mid=(lambda i=i: stats(i - 2)) if i >= 2 else None)
                conv_taps_dve(i)
                if i % 2 == 1:
                    w_release(1)
            conv(15, mid=lambda: stats(14))
            stats(15)
            if debug:
                for i in range(16):
                    P.dma("sp", (lambda i: lambda e: e.dma_start(out=dbg_GT[blk, i, :, 0:542], in_=GT[i][:, 0:542]))(i), reads=["GT%d" % i])
                    P.dma("sp", (lambda i: lambda e: e.dma_start(out=dbg_cT[blk, i, :, 0:ncol], in_=cT[i][:, 0:ncol]))(i),
                          reads=["cT%d" % i])
                P.dma("sp", lambda e: e.dma_start(out=dbg_hT[2 * blk + 1, :, :, 0:ncol], in_=hT[:, :, 0:ncol]),
                      reads=["hT%d" % t[0] for t in tiles_of(blk)])
            for (c0, n) in segs:
                if n == 512:
                    pS, pQ, rS, rQ = bank(6), bank(7), "pb6", "pb7"
                else:
                    pS, pQ, rS, rQ = sacc[:, 0:64], sacc[:, 64:128], "sacc", "sacc"
                P.c("dve", (lambda pS, c0, n: lambda e: e.tensor_scalar(out=mu_bc[:, c0:c0 + n], in0=pS, scalar1=1.0 / E,
                                                                        scalar2=None, op0=ALU.mult))(pS, c0, n),
                    reads=[rS], writes=["mu"])
                P.c("dve", (lambda c0, n: lambda e: e.tensor_tensor(out=var_t[:, c0:c0 + n], in0=mu_bc[:, c0:c0 + n],
                                                                    in1=mu_bc[:, c0:c0 + n], op=ALU.mult))(c0, n),
                    reads=["mu"], writes=["var"])
                P.c("dve", (lambda pQ, c0, n: lambda e: e.scalar_tensor_tensor(out=var_t[:, c0:c0 + n], in0=pQ, scalar=1.0 / E,
                                                                               in1=var_t[:, c0:c0 + n], op0=ALU.mult,
                                                                               op1=ALU.subtract))(pQ, c0, n),
                    reads=[rQ, "var"], writes=["var"])
            P.c("act", lambda e: e.activation(out=var_t[:, 0:ncol], in_=var_t[:, 0:ncol], func=AF.Sqrt, bias=epsc, scale=1.0),
                reads=["var", "epsc"], writes=["var"])
            P.c("dve", lambda e: e.reciprocal(out=rstd_bc[:, 0:ncol], in_=var_t[:, 0:ncol]), reads=["var"], writes=["rstd"])

            if last:
                for grp in range(4):
                    b = rotP.next()

                    def trp(e, b=b, grp=grp):
                        ins = None
                        for q in range(4):
                            i = grp * 4 + q
                            ins = e.transpose(bank(b)[0:30, q * 128:(q + 1) * 128], g32[i][:, 0:30], ident_f)
                        return ins
                    P.c("pe", trp, reads=["g32_%d" % (grp * 4 + q) for q in range(4)] + ["ident_f"], writes=["pb%d" % b])
                    P.c("act", (lambda b, grp: lambda e: e.activation(out=cstage[0:30, grp * 512:(grp + 1) * 512],
                                                                      in_=bank(b)[0:30, :], func=AF.Copy))(b, grp),
                        reads=["pb%d" % b], writes=CST)
                P.dma("sp", lambda e: e.dma_start(out=cp_d, in_=cstage[0:30, :]), reads=CST)
                for grp in range(4):
                    b = rotP.next()

                    def trs(e, b=b, grp=grp):
                        ins = None
                        for q in range(4):
                            i = grp * 4 + q
                            ins = e.transpose(bank(b)[0:64, q * 128:(q + 1) * 128], g32s[i], ident_f)
                        return ins
                    P.c("pe", trs, reads=["g32_%d" % (grp * 4 + q) for q in range(4)] + ["ident_f"], writes=["pb%d" % b])
                    P.c("act", (lambda b, grp: lambda e: e.activation(out=cstage[0:64, grp * 512:(grp + 1) * 512],
                                                                      in_=bank(b)[0:64, :], func=AF.Copy))(b, grp),
                        reads=["pb%d" % b], writes=CST)
                for bb in range(16):
                    P.dma("sp", (lambda bb: lambda e: e.dma_start(out=cs_d[bb, 26:30, :],
                                                                  in_=cstage[4 * bb:4 * bb + 4, :]))(bb),
                          reads=CST)

            if blk + 1 < NBLK:
                load_lnbc(["dgA", "dgB", "GTs0", "GTs1", "stg0", "stg1", "sig0", "sig1"])
            def p2_front(i):
                wt, wres = wslot(blk, 24 + i // 4)
                par = i % 2
                z_outs = proj_fm(blk, wt, wres, i % 4, segs, "z", rot=rotB)
                for (pa, pr, c0, n) in z_outs:
                    P.c("act", (lambda pa, c0, n: lambda e: e.activation(out=szT[par][:, c0:c0 + n], in_=pa, func=AF.Silu))(pa, c0, n),
                        reads=[pr], writes=["szT%d" % par])
                P.c("dve", lambda e: e.tensor_tensor(out=cn[par][:, 0:ncol], in0=cT[i][:, 0:ncol], in1=mu_bc[:, 0:ncol], op=ALU.subtract),
                    reads=["cT%d" % i, "mu"], writes=["cn%d" % par])
                P.c("dve", lambda e: e.tensor_tensor(out=cn[par][:, 0:ncol], in0=cn[par][:, 0:ncol], in1=rstd_bc[:, 0:ncol], op=ALU.mult),
                    reads=["cn%d" % par, "rstd"], writes=["cn%d" % par])
                P.c("act", lambda e: e.activation(out=cn[par][:, 0:ncol], in_=cn[par][:, 0:ncol], func=AF.Silu,
                                                  scale=pB[:, 16 + i:17 + i], bias=pB[:, 32 + i:33 + i]),
                    reads=["cn%d" % par, "pB"], writes=["cn%d" % par])

            def p2_back(i):
                par = i % 2
                P.c("dve", lambda e: e.tensor_tensor(out=cT[i][:, 0:ncol], in0=cn[par][:, 0:ncol], in1=szT[par][:, 0:ncol], op=ALU.mult),
                    reads=["cn%d" % par, "szT%d" % par], writes=["cT%d" % i])
            for i in range(16):
                p2_front(i)
                if i >= 1:
                    p2_back(i - 1)
                if i % 4 == 3:
                    w_release(1)
            p2_back(15)
            if debug:
                for i in range(16):
                    P.dma("sp", (lambda i: lambda e: e.dma_start(out=dbg_oB[blk, i, :, 0:ncol], in_=cT[i][:, 0:ncol]))(i),
                          reads=["cT%d" % i])
            return o_phase(1, blk, cT, lambda i: "cT%d" % i, next_pro, defer_T=True)

        for tl in tiles_of(0):
            prologue(0, 0, tl)
        pend = []
        for blk in range(NBLK):
            fence()
            layer_a(blk, lambda tl, blk=blk: (1, blk, tl), pendT=pend)
            fence()
            if blk + 1 < NBLK:
                nxt = {t[0]: t for t in tiles_of(blk + 1)}

                def npro(tl, blk=blk, nxt=nxt):
                    if tl[0] in nxt:
                        return (0, blk + 1, nxt[tl[0]])
                    return None
                pend = layer_b(blk, npro)
                if blk + 1 == NBLK - 1:
                    prologue_dma(0, blk + 1, nxt[4])
                    pend.append((0, blk + 1, nxt[4]))
            else:
                pend = layer_b(blk, lambda tl: None)

        P.finalize(lambda n: es.enter_context(nc.semaphore(n)))
        with nc.Block() as block:
            P.emit(block)
    return nc


def _weight_stream(a_w_in, a_w_out, b_w_in, b_w_out):
    units = np.empty((UNITS_PER_BLOCK, 128, 4096), np.float32)

    def colblk(Wk, c0, n):
        return Wk[:, :, c0:c0 + n].transpose(1, 0, 2).reshape(128, 8 * n)

    Wk = a_w_in[0].reshape(8, 128, 3 * E)
    for j in range(4):
        units[j] = colblk(Wk, E + 512 * j, 512)
    for g in range(8):
        subs = [colblk(Wk, base + 128 * dt, 128) for base, dt in ((0, 2 * g), (0, 2 * g + 1), (2 * E, 2 * g), (2 * E, 2 * g + 1))]
        units[4 + g] = np.concatenate(subs, axis=1)
    Wo = a_w_out[0].reshape(16, 128, D)
    for j in range(4):
        units[12 + j] = Wo[4 * j:4 * j + 4].transpose(1, 0, 2).reshape(128, 4096)
    Wk = b_w_in[0].reshape(8, 128, 3 * E)
    for i2 in range(8):
        subs = []
        for e_ in (2 * i2, 2 * i2 + 1):
            for base in (0, E):
                subs.append(colblk(Wk, base + 128 * e_, 128))
        units[16 + i2] = np.concatenate(subs, axis=1)
    for j in range(4):
        subs = [colblk(Wk, 2 * E + 128 * e_, 128) for e_ in range(4 * j, 4 * j + 4)]
        units[24 + j] = np.concatenate(subs, axis=1)
    Wo = b_w_out[0].reshape(16, 128, D)
    for j in range(4):
        units[28 + j] = Wo[4 * j:4 * j + 4].transpose(1, 0, 2).reshape(128, 4096)
    return units


_NC_CACHE = {}
_DEBUG = False


def kernel(x_prompt, x_sample, state_conv, pre_norm_g, post_norm_g,
           a_w_in, a_ln_g, a_ln_b, a_w_s, a_b_s, a_w_out,
           b_w_in, b_conv_w, b_conv_b, b_ln_g, b_ln_b, b_w_out):
    f = lambda a: np.ascontiguousarray(np.asarray(a, dtype=np.float32))
    x_prompt, x_sample, state_conv = f(x_prompt), f(x_sample), f(state_conv)
    wst = _weight_stream(f(a_w_in), f(a_w_out), f(b_w_in), f(b_w_out))
    gvec = f(np.concatenate([f(pre_norm_g), f(post_norm_g)], axis=0))
    lnA = f(np.stack([f(a_ln_g)[0], f(a_ln_b)[0]], axis=0))
    ws = f(a_w_s)[0]
    wsT = f(ws.transpose(2, 0, 1).reshape(128, 8 * 128))
    w4 = ws[:, :4, :4].transpose(2, 0, 1)
    wsTs = f(np.tile(w4, (16, 1, 16)).reshape(64, 8 * 64))
    bs = f(a_b_s)[0]
    bsA = f(bs.reshape(1, 8 * 128))
    bsS = f(np.tile(bs[:, :4], (1, 16)).reshape(1, 8 * 64))
    cwT = f(f(b_conv_w)[0].T.reshape(16, 128, CW).transpose(1, 0, 2).reshape(128, 16 * CW))
    colp = lambda v: f(v)[0].reshape(16, 128).T
    pB = f(np.concatenate([colp(b_conv_b), colp(b_ln_g), colp(b_ln_b)], axis=1))

    in_maps = []
    for c in range(N_CORES):
        stc = state_conv[0, 16 * c:16 * (c + 1)]
        stT = f(stc.transpose(2, 0, 1).reshape(16, 128, 480).transpose(1, 0, 2))
        in_maps.append({
            "xp": f(x_prompt[c]), "xs": f(x_sample[16 * c:16 * (c + 1)].reshape(NS, D)),
            "stT": stT, "st": f(stc), "wst": wst, "gvec": gvec, "lnA": lnA, "wsT": wsT, "wsTs": wsTs,
            "bsA": bsA, "bsS": bsS, "cwT": cwT, "pB": pB,
        })
    if "nc" not in _NC_CACHE:
        _NC_CACHE["nc"] = build_nc(debug=_DEBUG)
    nc = _NC_CACHE["nc"]
    res = run_bass_kernel_spmd(nc, in_maps, core_ids=list(range(N_CORES)))
    r = res.results
    y_prompt = np.stack([r[c]["yp"] for c in range(N_CORES)], axis=0).astype(np.float32)
    y_sample = np.concatenate([r[c]["ys"].reshape(16, 4, D) for c in range(N_CORES)], axis=0).astype(np.float32)
    conv_p = np.stack([r[c]["cp"] for c in range(N_CORES)], axis=0)[None].astype(np.float32)
    conv_s = np.concatenate([r[c]["cs"] for c in range(N_CORES)], axis=0)[None].astype(np.float32)
    chunk_v = np.concatenate([r[c]["cv"].reshape(16, 4, E) for c in range(N_CORES)], axis=0)[None].astype(np.float32)
    return (y_prompt, y_sample, conv_p, conv_s, chunk_v)
```

```python
from contextlib import ExitStack

import numpy as np
import concourse.bass as bass
import concourse.mybir as mybir
from concourse.bass_utils import run_bass_kernel_spmd

F32 = mybir.dt.float32
BF16 = mybir.dt.bfloat16
AF = mybir.ActivationFunctionType
ALU = mybir.AluOpType

N_CORES = 8
D = 1024
E = 2048
SEQ = 2048
NBLK = 4
TB = 512
NS = 64
CW = 31
EPS = 1e-6
RING = 8
UNITS_PER_BLOCK = 32
W_INFLIGHT = 8
NT_DVE = 5
NT_DVE_LAST = 2

COMPUTE = ("pe", "act", "dve", "pool")


class _Op:
    __slots__ = ("idx", "eng", "fn", "deps", "kind", "cnt", "sem", "semval", "waits", "prewait")

    def __init__(self, idx, eng, fn, kind):
        self.idx = idx
        self.eng = eng
        self.fn = fn
        self.kind = kind
        self.deps = []
        self.cnt = 0
        self.sem = None
        self.semval = 0
        self.waits = {}
        self.prewait = None


class Prog:
    def __init__(self, nc, n_dma_sems=8):
        self.nc = nc
        self.ops = []
        self.last_writer = {}
        self.readers = {}
        self.n_dma_sems = n_dma_sems
        self.sems = {}
        self.dma_sems = {}

    def _add(self, eng, fn, kind, reads, writes):
        o = _Op(len(self.ops), eng, fn, kind)
        deps = set()
        for r in reads:
            w = self.last_writer.get(r)
            if w is not None:
                deps.add(w)
        for w_ in writes:
            w = self.last_writer.get(w_)
            if w is not None:
                deps.add(w)
            for rd in self.readers.get(w_, ()):
                deps.add(rd)
        o.deps = list(deps)
        for r in reads:
            self.readers.setdefault(r, []).append(o)
        for w_ in writes:
            self.last_writer[w_] = o
            self.readers[w_] = []
        self.ops.append(o)
        return o

    def c(self, eng, fn, reads=(), writes=()):
        return self._add(eng, fn, "c", reads, writes)

    def dma(self, q, fn, reads=(), writes=(), semkey=None):
        o = self._add(q, fn, "d", reads, writes)
        o.sem = semkey
        return o

    def finalize(self, sem_alloc):
        for e in COMPUTE:
            self.sems[e] = sem_alloc("sem_" + e)
        cnt = {e: 0 for e in COMPUTE}
        rr = {}
        use = {}
        last = {}
        for o in self.ops:
            if o.kind == "c":
                cnt[o.eng] += 1
                o.cnt = cnt[o.eng]
            else:
                key = o.sem
                if key is None:
                    r = rr.get(o.eng, 0)
                    rr[o.eng] = r + 1
                    key = (o.eng, r % self.n_dma_sems)
                if key not in self.dma_sems:
                    self.dma_sems[key] = sem_alloc("dsem_%s_%s" % (key[0], key[1]))
                    use[key] = 0
                use[key] += 1
                o.sem = key
                o.semval = 16 * use[key]
                o.prewait = last.get(key)
                last[key] = o
        for o in self.ops:
            need = {}
            deps = list(o.deps)
            if o.kind == "d" and o.prewait is not None:
                deps.append(o.prewait)
            for d in deps:
                if d.kind == "c":
                    if d.eng == o.eng and o.kind == "c" and d.eng == "pe":
                        continue
                    k = ("c", d.eng)
                    need[k] = max(need.get(k, 0), d.cnt)
                else:
                    k = ("d", d.sem)
                    need[k] = max(need.get(k, 0), d.semval)
            o.waits = need
        seen = {}
        for o in self.ops:
            s = seen.setdefault(o.eng, {})
            w2 = {}
            for k, v in o.waits.items():
                if s.get(k, 0) >= v:
                    continue
                w2[k] = v
                s[k] = v
            o.waits = w2

    def emit(self, block):
        by_eng = {}
        for o in self.ops:
            by_eng.setdefault(o.eng, []).append(o)

        def run(name, eng):
            for o in by_eng.get(name, []):
                for k, v in o.waits.items():
                    if k[0] == "c":
                        eng.wait_ge(self.sems[k[1]], v)
                    else:
                        eng.wait_ge(self.dma_sems[k[1]], v)
                ins = o.fn(eng)
                if o.kind == "c":
                    ins.then_inc(self.sems[o.eng], 1)
                else:
                    ins.then_inc(self.dma_sems[o.sem], 16)
            fin = {}
            for o in by_eng.get(name, []):
                if o.kind == "d":
                    fin[o.sem] = max(fin.get(o.sem, 0), o.semval)
            for k, v in fin.items():
                eng.wait_ge(self.dma_sems[k], v)

        @block.tensor
        def _(e):
            run("pe", e)

        @block.scalar
        def _(e):
            run("act", e)

        @block.vector
        def _(e):
            run("dve", e)

        @block.gpsimd
        def _(e):
            run("pool", e)

        @block.sync
        def _(e):
            run("sp", e)


class Rot:
    def __init__(self, items):
        self.items = list(items)
        self.i = 0

    def next(self):
        v = self.items[self.i % len(self.items)]
        self.i += 1
        return v


def build_nc(debug=False):
    nc = bass.Bass("TRN2", target_bir_lowering=False)

    def din(name, shape):
        return nc.dram_tensor(name, list(shape), F32, kind="ExternalInput").ap()

    def dout(name, shape):
        return nc.dram_tensor(name, list(shape), F32, kind="ExternalOutput").ap()

    xp_d = din("xp", [SEQ, D])
    xs_d = din("xs", [NS, D])
    stT_d = din("stT", [128, 16, 480])
    st_d = din("st", [16, 30, E])
    wst_d = din("wst", [UNITS_PER_BLOCK, 128, 4096])
    gvec_d = din("gvec", [4, D])
    lnA_d = din("lnA", [2, E])
    wsT_d = din("wsT", [128, 8 * 128])
    wsTs_d = din("wsTs", [64, 8 * 64])
    bsA_d = din("bsA", [1, 8 * 128])
    bsS_d = din("bsS", [1, 8 * 64])
    cwT_d = din("cwT", [128, 16 * CW])
    pB_d = din("pB", [128, 48])

    yp_d = dout("yp", [SEQ, D])
    ys_d = dout("ys", [NS, D])
    cp_d = dout("cp", [30, E])
    cs_d = dout("cs", [16, 30, E])
    cv_d = dout("cv", [NS, E])
    if debug:
        dbg_x1 = dout("dbg_x1", [SEQ + NS, D])
        def dout16(name, shape):
            return nc.dram_tensor(name, list(shape), BF16, kind="ExternalOutput").ap()
        dbg_outT = dout16("dbg_outT", [NBLK, 16, 128, 576])
        dbg_hT = dout16("dbg_hT", [2 * NBLK, 128, 8, 576])
        dbg_vn = dout16("dbg_vn", [NBLK, 5, 128, E])
        dbg_GT = dout16("dbg_GT", [NBLK, 16, 128, 544])
        dbg_cT = dout16("dbg_cT", [NBLK, 16, 128, 576])
        dbg_oB = dout16("dbg_oB", [NBLK, 16, 128, 576])
        dbg_q = dout("dbg_q", [NBLK, 16, 128, 576])
        dbg_gu = dout("dbg_gu", [NBLK, 16, 128, 576])

    es = ExitStack()
    with es:
        total_f32 = nc.sbuf_bytes_remaining // 4 - 16
        S = es.enter_context(nc.sbuf_tensor("S", [128, total_f32], F32))
        PS = es.enter_context(nc.psum_tensor("PS", [128, 4096], F32))
        P = Prog(nc)

        cur = [0]

        def f32v(n):
            a = S[:, cur[0]:cur[0] + n]
            cur[0] += n
            return a

        def b16v(n):
            n2 = (n + 1) // 2
            a = S[:, cur[0]:cur[0] + n2].bitcast(BF16)
            cur[0] += n2
            return a

        xres = [f32v(D) for _ in range(5)]
        hT = b16v(8 * 576).rearrange("p (k t) -> p k t", t=576)
        hb = [b16v(D) for _ in range(2)]
        junk = b16v(D)
        ytmpb = [f32v(D) for _ in range(2)]
        gb_pre = f32v(D)
        gb_post = f32v(D)
        ident_b = b16v(128)
        ident_f = f32v(128)
        ones_b = b16v(128)
        WsT = b16v(8 * 128).rearrange("p (g t) -> p g t", t=128)
        WsTs = b16v(8 * 64).rearrange("p (g t) -> p g t", t=64)
        bias2 = b16v(8 * 128)
        bias2s = b16v(8 * 64)
        cw_b = b16v(16 * 32).rearrange("p (i k) -> p i k", k=32)
        halo = b16v(16 * 32).rearrange("p (i k) -> p i k", k=32)
        pB = f32v(48)
        epsc = f32v(1)
        sel0 = f32v(1)
        small = f32v(64)
        dummy = f32v(2)
        ring = [b16v(4096) for _ in range(RING)]
        cacc = [f32v(512) for _ in range(2)] if NT_DVE > 0 else None
        arena0 = cur[0]

        gv = [f32v(E) for _ in range(2)]
        vn = [b16v(E) for _ in range(5)]
        lnbc_lo = cur[0]
        lng_bc = f32v(E)
        lnb_bc = f32v(E)
        lnbc_hi = cur[0]
        outT = [b16v(576) for _ in range(16)]
        gu = [f32v(576) for _ in range(2)]
        szA = [f32v(576) for _ in range(2)]
        bnst = [f32v(24) for _ in range(2)]
        endA = cur[0]

        cur[0] = arena0
        GT = [b16v(544) for _ in range(16)]
        cT = [b16v(576) for _ in range(16)]
        p1_lo = cur[0]
        dg = b16v(32 * 128).rearrange("p (k j) -> p k j", j=128)
        GTs = [b16v(544).rearrange("p (b t) -> p b t", t=34) for _ in range(2)]
        stg = [f32v(480) for _ in range(2)]
        sig = [f32v(576) for _ in range(2)]
        p1_hi = cur[0]
        assert p1_lo <= lnbc_lo and lnbc_hi <= p1_hi, (p1_lo, lnbc_lo, lnbc_hi, p1_hi)
        csq = [b16v(576) for _ in range(2)]
        mu_bc = f32v(576)
        rstd_bc = f32v(576)
        var_t = f32v(576)
        cst_lo = cur[0]
        szT = [f32v(576) for _ in range(2)]
        cn = [f32v(576) for _ in range(2)]
        cstage = S[:, cst_lo:cst_lo + E]
        g32 = [f32v(32) for _ in range(16)]
        g32s = [f32v(64) for _ in range(16)]
        sacc = f32v(128)
        endB = cur[0]
        assert max(endA, endB) <= total_f32, (endA, endB, total_f32)

        arenaA_res = (["gv0", "gv1", "lnbc", "bnst0", "bnst1", "gu0", "gu1", "szA0", "szA1"]
                      + ["vn%d" % j for j in range(5)] + ["outT%d" % i for i in range(16)])
        arenaB_res = (["GT%d" % i for i in range(16)] + ["cT%d" % i for i in range(16)]
                      + ["dgA", "dgB", "GTs0", "GTs1", "stg0", "stg1", "sig0", "sig1", "csq0", "csq1",
                         "mu", "rstd", "var", "szT0", "szT1", "cn0", "cn1"]
                      + ["g32_%d" % i for i in range(16)] + ["sacc"])

        CST = ["szT0", "szT1", "cn0", "cn1"]

        def fence():
            P.c("dve", lambda e: e.memset(dummy[:, 0:1], 0.0), writes=arenaA_res + arenaB_res + ["dummy"])

        def bank(b):
            return PS[:, 512 * b:512 * (b + 1)]

        def slot(k):
            return PS[:, 512 * 3 + 64 * k:512 * 3 + 64 * (k + 1)]

        rotP = Rot([0, 1, 2])
        rotP4 = Rot([0, 1, 2, 3])
        rotM = Rot([4, 5])
        rotY = Rot([(6, 7), (4, 5)])
        srotP = Rot([0, 1, 2, 3])
        srotM = Rot([4, 5])

        wstate = {"issued": 0, "released": 0}
        total_units = NBLK * UNITS_PER_BLOCK

        def w_issue():
            while wstate["issued"] < total_units and wstate["issued"] - RING < wstate["released"]:
                n = wstate["issued"]
                sl = n % RING
                u = n % UNITS_PER_BLOCK
                P.dma("pool", (lambda sl, u: lambda e: e.dma_start(out=ring[sl], in_=wst_d[u]))(sl, u),
                      writes=["w%d" % sl, "wthr%d" % (n % W_INFLIGHT)], semkey=("w", sl))
                wstate["issued"] += 1

        def w_release(k):
            wstate["released"] += k
            w_issue()

        def wslot(blk, u):
            n = blk * UNITS_PER_BLOCK + u
            assert n < wstate["issued"], (n, wstate)
            return ring[n % RING], "w%d" % (n % RING)

        P.c("pool", lambda e: e.memset(ident_f, 1.0), writes=["ident_f"])
        P.c("pool", lambda e: e.affine_select(out=ident_f, in_=ident_f, pattern=[[-1, 128]],
                                               compare_op=ALU.is_equal, fill=0.0, base=0,
                                               channel_multiplier=1), reads=["ident_f"], writes=["ident_f"])
        P.c("pool", lambda e: e.tensor_copy(out=ident_b, in_=ident_f), reads=["ident_f"], writes=["ident_b"])
        P.c("pool", lambda e: e.memset(ones_b, 1.0), writes=["ones_b"])
        P.c("pool", lambda e: e.memset(epsc, EPS), writes=["epsc"])
        P.c("pool", lambda e: e.memset(halo, 0.0), writes=["halo%d" % i for i in range(16)])
        P.c("pool", lambda e: e.memset(sel0, 1.0), writes=["sel0"])
        P.c("pool", lambda e: e.affine_select(out=sel0, in_=sel0, pattern=[[0, 1]], compare_op=ALU.is_equal,
                                               fill=0.0, base=0, channel_multiplier=1),
            reads=["sel0"], writes=["sel0"])
        w_issue()

        def load_post(layer):
            P.dma("sp", lambda e: e.dma_start(out=gb_post, in_=gvec_d[2 + layer:3 + layer, :].broadcast_to([128, D])),
                  writes=["gb_post"])

        def load_pre(nl):
            P.dma("sp", lambda e: e.dma_start(out=gb_pre, in_=gvec_d[nl:nl + 1, :].broadcast_to([128, D])),
                  writes=["gb_pre"])
        P.dma("sp", lambda e: e.dma_start(out=gb_pre, in_=gvec_d[0:1, :].broadcast_to([128, D])), writes=["gb_pre"])
        P.dma("sp", lambda e: e.dma_start(out=pB, in_=pB_d), writes=["pB"])
        cur[0] = arena0
        st_cw = f32v(16 * CW)
        st_ws = f32v(1024)
        st_wss_full = f32v(512)
        setup_tmp = ["st_cw", "st_ws", "st_wss"]
        P.dma("sp", lambda e: e.dma_start(out=st_cw, in_=cwT_d), writes=["st_cw"])
        P.c("dve", lambda e: e.memset(cw_b, 0.0), writes=["cw_b"])
        P.c("dve", lambda e: e.tensor_copy(out=cw_b[:, :, 0:CW], in_=st_cw.rearrange("p (i k) -> p i k", k=CW)),
            reads=["st_cw"], writes=["cw_b"])
        P.dma("sp", lambda e: e.dma_start(out=st_ws, in_=wsT_d), writes=["st_ws"])
        ws3 = st_ws.rearrange("p (g t) -> p g t", t=128)
        P.c("pool", lambda e: e.affine_select(out=ws3, in_=ws3, pattern=[[0, 8], [1, 128]], compare_op=ALU.is_ge,
                                               fill=0.0, base=0, channel_multiplier=-1),
            reads=["st_ws"], writes=["st_ws"])
        P.c("pool", lambda e: e.tensor_copy(out=WsT, in_=ws3), reads=["st_ws"], writes=["WsT"])
        st_wss = st_wss_full[0:64, :]
        P.dma("sp", lambda e: e.dma_start(out=st_wss, in_=wsTs_d), writes=["st_wss"])
        wss4 = st_wss.rearrange("p (g b t) -> p g b t", b=16, t=4)
        P.c("pool", lambda e: e.affine_select(out=wss4, in_=wss4, pattern=[[0, 8], [4, 16], [1, 4]],
                                               compare_op=ALU.is_ge, fill=0.0, base=0, channel_multiplier=-1),
            reads=["st_wss"], writes=["st_wss"])
        P.c("pool", lambda e: e.affine_select(out=wss4, in_=wss4, pattern=[[0, 8], [-4, 16], [0, 4]],
                                               compare_op=ALU.is_ge, fill=0.0, base=0, channel_multiplier=1),
            reads=["st_wss"], writes=["st_wss"])
        P.c("pool", lambda e: e.tensor_copy(out=WsTs[0:64], in_=st_wss.rearrange("p (g t) -> p g t", t=64)),
            reads=["st_wss"], writes=["WsTs"])

        def bias_rows(src_d, n, dst, tag):
            t32 = f32v(n)[0:2, :]
            thi = b16v(n)[0:2, :]
            tlo = b16v(n)[0:2, :]
            td = f32v(n)[0:2, :]
            r = "bias_" + tag
            setup_tmp.extend([r, r + "h", r + "l", r + "d"])
            P.dma("sp", lambda e: e.dma_start(out=t32, in_=src_d.broadcast_to([2, n])), writes=[r])
            P.c("dve", lambda e: e.tensor_copy(out=thi, in_=t32), reads=[r], writes=[r + "h"])
            P.c("dve", lambda e: e.tensor_tensor(out=td, in0=t32, in1=thi, op=ALU.subtract),
                reads=[r, r + "h"], writes=[r + "d"])
            P.c("dve", lambda e: e.tensor_copy(out=tlo, in_=td), reads=[r + "d"], writes=[r + "l"])
            P.c("dve", lambda e: e.tensor_tensor(out=td, in0=thi, in1=tlo, op=ALU.subtract),
                reads=[r + "h", r + "l", r + "d"], writes=[r + "d"])
            P.c("dve", lambda e: e.scalar_tensor_tensor(out=dst[0:2, :], in0=td, scalar=sel0[0:2, :], in1=tlo,
                                                         op0=ALU.mult, op1=ALU.add),
                reads=[r + "d", r + "l", "sel0"], writes=[tag])

        bias_rows(bsA_d, 1024, bias2, "bias2")
        bias_rows(bsS_d, 512, bias2s, "bias2s")
        assert cur[0] <= total_f32
        P.c("dve", lambda e: e.memset(dummy[:, 1:2], 0.0), writes=setup_tmp + arenaA_res + arenaB_res + ["dummy2"])

        P.dma("sp", lambda e: e.dma_start(out=cs_d[:, 0:26, :], in_=st_d[:, 4:30, :]))

        def tiles_of(blk):
            t = [(j, 128, j * 128) for j in range(4)]
            if blk == NBLK - 1:
                t.append((4, NS, 512))
            return t

        def segs_of(blk):
            s = [(0, 512)]
            if blk == NBLK - 1:
                s.append((512, NS))
            return s

        def hT_res(c0, n):
            return ["hT%d" % j for j in range(c0 // 128, (c0 + n + 127) // 128)]

        SD_COL = {"pro": 8, "epi": 9}

        def rsqrt_chain(ss, nr, scale, rs_out, tag):
            sd = small[:, SD_COL[tag]:SD_COL[tag] + 1]
            P.c("act", lambda e: e.activation(out=sd[:nr], in_=ss[:nr], func=AF.Sqrt, bias=epsc[:nr], scale=scale),
                reads=[tag + "_ss", "epsc"], writes=[tag + "_sd"])
            P.c("dve", lambda e: e.reciprocal(out=rs_out[:nr], in_=sd[:nr]), reads=[tag + "_sd"], writes=[tag + "_rs"])

        def prologue_dma(layer, blk, tl):
            j, nr, c0 = tl
            xr = "xres%d" % j
            if layer == 0:
                src = xs_d if j == 4 else xp_d[blk * TB + j * 128: blk * TB + (j + 1) * 128, :]
                P.dma("sp", lambda e: e.dma_start(out=xres[j][:nr], in_=src), writes=[xr])

        def prologue_pre(layer, blk, tl):
            j, nr, c0 = tl
            xr = "xres%d" % j
            ss = small[:, 0:1]
            rs = small[:, 1:2]
            hbuf = hb[j % 2]
            hres = "hb%d" % (j % 2)
            P.c("act", lambda e: e.activation(out=junk[:nr], in_=xres[j][:nr], func=AF.Square, accum_out=ss[:nr]),
                reads=[xr], writes=["junk", "pro_ss"])
            rsqrt_chain(ss, nr, 1.0 / D, rs, "pro")
            P.c("dve", lambda e: e.scalar_tensor_tensor(out=hbuf[:nr], in0=xres[j][:nr], scalar=rs[:nr],
                                                         in1=gb_pre[:nr], op0=ALU.mult, op1=ALU.mult),
                reads=[xr, "pro_rs", "gb_pre"], writes=[hres])

        def prologue_T(layer, blk, tl):
            j, nr, c0 = tl
            hbuf = hb[j % 2]
            hres = "hb%d" % (j % 2)
            b = rotP.next()
            pb = bank(b).bitcast(BF16).rearrange("p (k t) -> p k t", t=128)

            def tr(e):
                ins = None
                for k in range(8):
                    ins = e.transpose(pb[:, k, 0:nr], hbuf[:nr, k * 128:(k + 1) * 128], ident_b[:nr, :nr])
                return ins
            P.c("pe", tr, reads=[hres, "ident_b"], writes=["pb%d" % b])
            P.c("act", lambda e: e.activation(out=hT[:, :, c0:c0 + nr], in_=pb[:, :, 0:nr], func=AF.Copy),
                reads=["pb%d" % b], writes=["hT%d" % j])

        def prologue(layer, blk, tl):
            prologue_dma(layer, blk, tl)
            prologue_pre(layer, blk, tl)
            prologue_T(layer, blk, tl)

        def proj_fm(blk, wt, wres, sub, segs, tag, rot=None):
            outs = []
            for (c0, n) in segs:
                b = (rot or rotP).next()
                pa, pr = bank(b)[:, 0:n], "pb%d" % b

                def mm(e, pa=pa, c0=c0, n=n):
                    ins = None
                    for k in range(8):
                        ins = e.matmul(pa, lhsT=wt[:, sub * 1024 + k * 128: sub * 1024 + (k + 1) * 128],
                                       rhs=hT[:, k, c0:c0 + n], start=(k == 0), stop=(k == 7))
                    return ins
                P.c("pe", mm, reads=[wres] + hT_res(c0, n), writes=[pr])
                outs.append((pa, pr, c0, n))
            return outs

        def o_phase(layer, blk, outbuf, outres_fn, next_pro, defer_T=False):
            tls = tiles_of(blk)
            ubase = 12 if layer == 0 else 28
            def o_tile(tl, k, tgts):
                j, nr, c0 = tl
                bks = rotY.next()

                yres = ["pb%d" % bks[0], "pb%d" % bks[1]]
                for g4 in range(4):
                    def mm(e, bks=bks, nr=nr, c0=c0, g4=g4):
                        ins = None
                        wt, _ = wslot(blk, ubase + g4)
                        for kt in range(4 * g4, 4 * g4 + 4):
                            for h in range(2):
                                ins = e.matmul(bank(bks[h])[:nr, :], lhsT=outbuf[kt][:, c0:c0 + nr],
                                               rhs=wt[:, (kt % 4) * 1024 + h * 512:(kt % 4) * 1024 + (h + 1) * 512],
                                               start=(kt == 0), stop=(kt == 15))
                        return ins
                    P.c("pe", mm, reads=[wslot(blk, ubase + g4)[1]] + [outres_fn(i) for i in range(4 * g4, 4 * g4 + 4)],
                        writes=yres)
                    if k == len(tls) - 1:
                        w_release(1)
                if k >= 2 and tgts[k - 2] is not None:
                    prologue_T(*tgts[k - 2])
                rs = small[:, 4:5]
                sst = small[:, 5:6]
                assert bks[1] == bks[0] + 1
                y2 = PS[:, 512 * bks[0]:512 * bks[0] + 1024]
                P.c("act", lambda e: e.activation(out=junk[:nr, :], in_=y2[:nr, :], func=AF.Square, accum_out=sst[:nr]),
                    reads=yres, writes=["junk", "epi_ss"])
                rsqrt_chain(sst, nr, 1.0 / D, rs, "epi")
                ytmp = ytmpb[j % 2]
                yt = "ytmp%d" % (j % 2)
                P.c("dve", lambda e: e.scalar_tensor_tensor(out=ytmp[:nr, :], in0=y2[:nr, :], scalar=rs[:nr],
                                                            in1=gb_post[:nr, :], op0=ALU.mult, op1=ALU.mult),
                    reads=yres + ["epi_rs", "gb_post"], writes=[yt + "_0", yt + "_1", yt])
                xr = "xres%d" % j
                if layer == 0:
                    P.c("dve", lambda e: e.tensor_tensor(out=xres[j][:nr], in0=xres[j][:nr], in1=ytmp[:nr], op=ALU.add),
                        reads=[xr, yt + "_0", yt + "_1"], writes=[xr])
                else:
                    P.c("dve", lambda e: e.tensor_tensor(out=ytmp[:nr], in0=xres[j][:nr], in1=ytmp[:nr], op=ALU.add),
                        reads=[xr, yt + "_0", yt + "_1"], writes=[yt + "_0", yt + "_1", yt])
                if layer == 0 and debug:
                    ddst = dbg_x1[SEQ:SEQ + NS, :] if j == 4 else dbg_x1[blk * TB + j * 128: blk * TB + (j + 1) * 128, :]
                    P.dma("sp", lambda e: e.dma_start(out=ddst, in_=xres[j][:nr]), reads=[xr])
                tgt = next_pro(tl)
                if tgt is not None:
                    prologue_dma(*tgt)
                if layer == 1:
                    dst = ys_d if j == 4 else yp_d[blk * TB + j * 128: blk * TB + (j + 1) * 128, :]
                    P.dma("sp", lambda e: e.dma_start(out=dst, in_=ytmp[:nr]), reads=[yt])
                return tgt
            tgts = []
            n = len(tls)
            for k, tl in enumerate(tls):
                tgts.append(o_tile(tl, k, tgts))
                if k >= 1 and tgts[k - 1] is not None and not (defer_T and k - 1 >= n - 2):
                    prologue_pre(*tgts[k - 1])
            pend = [tgts[k] for k in (n - 2, n - 1) if k >= 0 and tgts[k] is not None]
            if defer_T:
                return pend
            if tgts[n - 1] is not None:
                prologue_pre(*tgts[n - 1])
            for tgt in pend:
                prologue_T(*tgt)
            return []

        def load_state(i, extra):
            par = i % 2
            P.dma("sp", lambda e: e.dma_start(out=stg[par], in_=stT_d[:, i, :]), writes=["stg%d" % par] + extra)

        def load_lnbc(extra):
            P.dma("sp", lambda e: e.dma_start(out=lng_bc, in_=lnA_d[0:1, :].broadcast_to([128, E])), writes=["lnbc"] + extra)
            P.dma("sp", lambda e: e.dma_start(out=lnb_bc, in_=lnA_d[1:2, :].broadcast_to([128, E])), writes=["lnbc"] + extra)

        def layer_a(blk, next_pro, pendT=()):
            tls = tiles_of(blk)
            segs = segs_of(blk)
            load_post(0)
            ncolA = 512 + (NS if blk == NBLK - 1 else 0)
            if blk == 0:
                load_lnbc([])
            def v_tile(j, nr, c0, after_cb0=None):
                g_ = gv[j % 2]
                gres = "gv%d" % (j % 2)
                st_ = bnst[j % 2]
                sres = "bnst%d" % (j % 2)
                for cb in range(4):
                    wt, wres = wslot(blk, cb)
                    b = rotP.next()

                    def mm(e, b=b, wt=wt, nr=nr, c0=c0):
                        ins = None
                        for k in range(8):
                            ins = e.matmul(bank(b)[:nr, :], lhsT=hT[:, k, c0:c0 + nr],
                                           rhs=wt[:, k * 512:(k + 1) * 512], start=(k == 0), stop=(k == 7))
                        return ins
                    P.c("pe", mm, reads=[wres, "hT%d" % j], writes=["pb%d" % b])
                    P.c("act", (lambda b, cb: lambda e: e.activation(out=g_[:nr, cb * 512:(cb + 1) * 512],
                                                                     in_=bank(b)[:nr, :], func=AF.Gelu))(b, cb),
                        reads=["pb%d" % b], writes=[gres + "_%d" % cb, gres])
                    P.c("dve", (lambda cb: lambda e: e.bn_stats(out=st_[:nr, cb * 6:(cb + 1) * 6],
                                                                in_=g_[:nr, cb * 512:(cb + 1) * 512]))(cb),
                        reads=[gres + "_%d" % cb], writes=[sres + "_%d" % cb])
                    if cb == 0 and after_cb0 is not None:
                        after_cb0()
                mv = small[:, 16 + 8 * (j % 2):18 + 8 * (j % 2)]
                P.c("dve", lambda e: e.bn_aggr(out=mv[:nr], in_=st_[:nr, 0:24]),
                    reads=[sres + "_%d" % cb for cb in range(4)], writes=["ln_mv%d" % (j % 2), sres])

            def v_tail_a(j, nr, c0):
                q = j % 2
                mv = small[:, 16 + 8 * q:18 + 8 * q]
                rs = small[:, 18 + 8 * q:19 + 8 * q]
                nb = small[:, 19 + 8 * q:20 + 8 * q]
                sd = small[:, 20 + 8 * q:21 + 8 * q]
                P.c("act", lambda e: e.activation(out=sd[:nr], in_=mv[:nr, 1:2], func=AF.Sqrt, bias=epsc[:nr], scale=1.0),
                    reads=["ln_mv%d" % q, "epsc"], writes=["ln_sd%d" % q])
                P.c("dve", lambda e: e.reciprocal(out=rs[:nr], in_=sd[:nr]), reads=["ln_sd%d" % q], writes=["ln_rs%d" % q])
                P.c("dve", lambda e: e.scalar_tensor_tensor(out=nb[:nr], in0=mv[:nr, 0:1], scalar=-1.0, in1=rs[:nr],
                                                             op0=ALU.mult, op1=ALU.mult),
                    reads=["ln_mv%d" % q, "ln_rs%d" % q], writes=["ln_nb%d" % q])

            def v_tail(j, nr, c0):
                g_ = gv[j % 2]
                gres = "gv%d" % (j % 2)
                q = j % 2
                rs = small[:, 18 + 8 * q:19 + 8 * q]
                nb = small[:, 19 + 8 * q:20 + 8 * q]
                allg = [gres + "_%d" % cb for cb in range(4)]
                P.c("act", lambda e: e.activation(out=g_[:nr], in_=g_[:nr], func=AF.Identity, scale=rs[:nr], bias=nb[:nr]),
                    reads=allg + ["ln_rs%d" % q, "ln_nb%d" % q], writes=allg + [gres])
                P.c("dve", lambda e: e.tensor_tensor(out=g_[:nr], in0=g_[:nr], in1=lng_bc[:nr], op=ALU.mult),
                    reads=[gres, "lnbc"], writes=[gres])
                if j == 4:
                    P.c("dve", lambda e: e.tensor_tensor(out=g_[:nr], in0=g_[:nr], in1=lnb_bc[:nr], op=ALU.add),
                        reads=[gres, "lnbc"], writes=[gres])
                    P.dma("sp", lambda e: e.dma_start(out=cv_d, in_=g_[:nr]), reads=[gres])
                    P.c("dve", lambda e: e.tensor_copy(out=vn[j][:nr], in_=g_[:nr]), reads=[gres], writes=["vn%d" % j])
                else:
                    P.c("dve", lambda e: e.tensor_tensor(out=vn[j][:nr], in0=g_[:nr], in1=lnb_bc[:nr], op=ALU.add),
                        reads=[gres, "lnbc"], writes=["vn%d" % j])
            pendT = list(pendT)
            pendP = list(pendT)
            for idx, (j, nr, c0) in enumerate(tls):
                while pendT and pendT[0][2][0] <= j:
                    prologue_T(*pendT.pop(0))

                def hook(idx=idx, j=j):
                    if idx >= 1:
                        v_tail_a(*tls[idx - 1])
                    while pendP and pendP[0][2][0] <= j + 1:
                        prologue_pre(*pendP.pop(0))
                v_tile(j, nr, c0, after_cb0=hook)
                if idx >= 1:
                    v_tail(*tls[idx - 1])
            v_tail_a(*tls[-1])
            v_tail(*tls[-1])
            load_pre(1)
            if blk == NBLK - 1:
                load_state(0, ["lnbc"])
                load_state(1, ["lnbc"])
            w_release(4)
            pend = []

            def mix_and_gate(g, items):
                for (dt, us, zs, par) in items:
                    mouts = []
                    for (c0, n) in segs:
                        if n == 512:
                            b = rotM.next()
                            pa, pr = bank(b), "pb%d" % b

                            def mm(e, pa=pa, dt=dt, g=g):
                                rb = bias2[0:2, g * 128:(g + 1) * 128].unsqueeze(1).broadcast_to([2, 4, 128])
                                e.matmul(pa, lhsT=ones_b[0:2, :], rhs=rb, start=True, stop=False)
                                ins = None
                                for j in range(4):
                                    ins = e.matmul(pa[:, j * 128:(j + 1) * 128], lhsT=vn[j][:, dt * 128:(dt + 1) * 128],
                                                   rhs=WsT[:, g, :], start=False, stop=(j == 3))
                                return ins
                            P.c("pe", mm, reads=["bias2", "ones_b", "WsT"] + ["vn%d" % j for j in range(4)], writes=[pr])
                        else:
                            b = rotM.next()
                            pa, pr = bank(b)[:, 0:n], "pb%d" % b

                            def mm(e, pa=pa, dt=dt, g=g):
                                e.matmul(pa, lhsT=ones_b[0:2, :], rhs=bias2s[0:2, g * 64:(g + 1) * 64],
                                         start=True, stop=False)
                                return e.matmul(pa, lhsT=vn[4][0:64, dt * 128:(dt + 1) * 128], rhs=WsTs[0:64, g, :],
                                                start=False, stop=True)
                            P.c("pe", mm, reads=["bias2s", "ones_b", "WsTs", "vn4"], writes=[pr])
                        mouts.append((pa, pr, c0, n))
                    gures, szres = "gu%d" % par, "szA%d" % par
                    P.c("dve", (lambda par: lambda e: e.tensor_tensor(out=gu[par][:, 0:ncolA], in0=gu[par][:, 0:ncolA],
                                                                      in1=szA[par][:, 0:ncolA], op=ALU.mult))(par),
                        reads=[gures, szres], writes=[gures])
                    if debug:
                        P.dma("sp", (lambda dt, par: lambda e: e.dma_start(out=dbg_q[blk, dt, :, 0:ncolA], in_=gu[par][:, 0:ncolA]))(dt, par),
                              reads=[gures])
                    for (pa, pr, c0, n) in mouts:
                        P.c("dve", (lambda pa, c0, n, dt, par: lambda e: e.tensor_tensor(
                            out=outT[dt][:, c0:c0 + n], in0=pa, in1=gu[par][:, c0:c0 + n], op=ALU.mult))(pa, c0, n, dt, par),
                            reads=[pr, gures], writes=["outT%d" % dt])

            for g in range(8):
                wt, wres = wslot(blk, 4 + g)
                items = []
                us_all, zs_all = [], []
                for q in range(2):
                    us_all.append(proj_fm(blk, wt, wres, q, segs, "u", rot=rotP4))
                for q in range(2):
                    dt = 2 * g + q
                    par = dt % 2
                    for (pa, pr, c0, n) in us_all[q]:
                        P.c("act", (lambda pa, c0, n, par: lambda e: e.activation(out=gu[par][:, c0:c0 + n], in_=pa, func=AF.Gelu))(pa, c0, n, par),
                            reads=[pr], writes=["gu%d" % par])
                for q in range(2):
                    zs_all.append(proj_fm(blk, wt, wres, 2 + q, segs, "z", rot=rotP4))
                for q in range(2):
                    dt = 2 * g + q
                    par = dt % 2
                    for (pa, pr, c0, n) in zs_all[q]:
                        P.c("act", (lambda pa, c0, n, par: lambda e: e.activation(out=szA[par][:, c0:c0 + n], in_=pa, func=AF.Silu))(pa, c0, n, par),
                            reads=[pr], writes=["szA%d" % par])
                    items.append((dt, us_all[q], zs_all[q], par))
                    if debug:
                        P.dma("sp", (lambda dt, par: lambda e: e.dma_start(out=dbg_gu[blk, dt, :, 0:ncolA], in_=gu[par][:, 0:ncolA]))(dt, par),
                              reads=["gu%d" % par])
                mix_and_gate(g, items)
                w_release(1)
            if debug:
                nc_ = ncolA
                for i in range(16):
                    P.dma("sp", (lambda i: lambda e: e.dma_start(out=dbg_outT[blk, i, :, 0:nc_], in_=outT[i][:, 0:nc_]))(i),
                          reads=["outT%d" % i])
                for (j, nr, c0) in tls:
                    P.dma("sp", (lambda j, nr: lambda e: e.dma_start(out=dbg_vn[blk, j, 0:nr, :], in_=vn[j][:nr]))(j, nr),
                          reads=["vn%d" % j])
                P.dma("sp", lambda e: e.dma_start(out=dbg_hT[2 * blk, :, :, 0:nc_], in_=hT[:, :, 0:nc_]),
                      reads=["hT%d" % t[0] for t in tls])
            o_phase(0, blk, outT, lambda i: "outT%d" % i, next_pro)

        def layer_b(blk, next_pro):
            segs = segs_of(blk)
            last = (blk == NBLK - 1)
            load_post(1)
            load_pre(0)
            rotB = rotP4
            ncol = 512 + (NS if last else 0)
            stat_b = (6, 7)

            ntv = NT_DVE_LAST if last else NT_DVE
            kpe0 = 0 if last else ntv
            ksplit = 16 if last else (ntv + (32 - ntv) // 2)

            def build_diag(i, half):
                k0, k1 = (kpe0, ksplit) if half == 0 else (ksplit, 32)
                in0 = ident_b.unsqueeze(1).broadcast_to([128, k1 - k0, 128])
                in1 = cw_b[:, i, k0:k1].unsqueeze(2).broadcast_to([128, k1 - k0, 128])
                P.c("dve", lambda e: e.tensor_tensor(out=dg[:, k0:k1, :], in0=in0, in1=in1, op=ALU.mult),
                    reads=["ident_b", "cw_b"], writes=["dgA" if half == 0 else "dgB"])

            def glu_pre(i):
                par = i % 2
                P.c("act", lambda e: e.activation(out=GT[i][:, 0:30], in_=halo[:, i, 0:30], func=AF.Copy),
                    reads=["halo%d" % i], writes=["GT%d" % i])
                if last:
                    P.c("act", lambda e: e.activation(out=GTs[par][:, :, 0:30],
                                                      in_=stg[par].rearrange("p (b t) -> p b t", t=30), func=AF.Copy),
                        reads=["stg%d" % par], writes=["GTs%d" % par])
                    if i + 2 < 16:
                        load_state(i + 2, [])

            def glu(i, a_outs, gl_outs):
                par = i % 2
                for (pa, pr, c0, n) in gl_outs:
                    P.c("act", (lambda pa, c0, n: lambda e: e.activation(out=sig[par][:, c0:c0 + n], in_=pa, func=AF.Sigmoid))(pa, c0, n),
                        reads=[pr], writes=["sig%d" % par])
                for (pa, pr, c0, n) in a_outs:
                    if n == 512:
                        P.c("dve", (lambda pa: lambda e: e.tensor_tensor(out=GT[i][:, 30:542], in0=pa, in1=sig[par][:, 0:512], op=ALU.mult))(pa),
                            reads=[pr, "sig%d" % par], writes=["GT%d" % i])
                        if last:
                            P.c("dve", (lambda pa: lambda e: e.tensor_tensor(out=g32[i][:, 0:30], in0=pa[:, 482:512],
                                                                             in1=sig[par][:, 482:512], op=ALU.mult))(pa),
                                reads=[pr, "sig%d" % par], writes=["g32_%d" % i])
                    else:
                        P.c("dve", (lambda pa: lambda e: e.tensor_tensor(out=g32s[i], in0=pa, in1=sig[par][:, 512:576], op=ALU.mult))(pa),
                            reads=[pr, "sig%d" % par], writes=["g32_%d" % i])
                        P.c("dve", lambda e: e.tensor_copy(out=GTs[par][:, :, 30:34],
                                                           in_=g32s[i].rearrange("p (b t) -> p b t", t=4)),
                            reads=["g32_%d" % i], writes=["GTs%d" % par])

            def conv(i, mid=None):
                par = i % 2
                couts = []
                for (c0, n) in segs:
                    b = rotM.next()
                    pa, pr = bank(b)[:, 0:n], "pb%d" % b
                    couts.append((pa, pr, c0, n))
                for half in range(2):
                    if half == 1 and mid is not None:
                        mid()
                    k0, k1 = (kpe0, ksplit) if half == 0 else (ksplit, CW)
                    for (pa, pr, c0, n) in couts:
                        def mm(e, pa=pa, n=n, k0=k0, k1=k1):
                            ins = None
                            kf = ntv if n == 512 else 0
                            for k in range(max(k0, kf), k1):
                                rhs = GT[i][:, k:k + 512] if n == 512 else GTs[par][:, :, k:k + 4]
                                ins = e.matmul(pa, lhsT=dg[:, k, :], rhs=rhs, start=(k == kf), stop=(k == CW - 1))
                            return ins
                        src = "GT%d" % i if n == 512 else "GTs%d" % par
                        P.c("pe", mm, reads=["dgA" if half == 0 else "dgB", src], writes=[pr])
                if i + 1 < 16:
                    build_diag(i + 1, 0)
                    build_diag(i + 1, 1)
                P.c("act", lambda e: e.activation(out=halo[:, i, 0:30], in_=GT[i][:, 512:542], func=AF.Copy),
                    reads=["GT%d" % i], writes=["halo%d" % i])
                for (pa, pr, c0, n) in couts:
                    if n == 512 and ntv > 0:
                        P.c("dve", (lambda pa: lambda e: e.scalar_tensor_tensor(out=cT[i][:, 0:512], in0=pa, scalar=pB[:, i:i + 1],
                                                                               in1=cacc[par], op0=ALU.add, op1=ALU.add))(pa),
                            reads=[pr, "pB", "cacc%d" % par], writes=["cT%d" % i])
                        P.c("act", lambda e: e.activation(out=csq[par][:, 0:512], in_=cT[i][:, 0:512], func=AF.Square),
                            reads=["cT%d" % i], writes=["csq%d" % par])
                    else:
                        P.c("act", (lambda pa, c0, n: lambda e: e.activation(out=cT[i][:, c0:c0 + n], in_=pa, func=AF.Identity,
                                                                             bias=pB[:, i:i + 1], scale=1.0))(pa, c0, n),
                            reads=[pr, "pB"], writes=["cT%d" % i])
                        P.c("act", (lambda pa, c0, n: lambda e: e.activation(out=csq[par][:, c0:c0 + n], in_=pa, func=AF.Square,
                                                                             bias=pB[:, i:i + 1], scale=1.0))(pa, c0, n),
                            reads=[pr, "pB"], writes=["csq%d" % par])

            def conv_taps_dve(i):
                par = i % 2
                if ntv == 0:
                    return
                P.c("dve", lambda e: e.tensor_scalar(out=cacc[par], in0=GT[i][:, 0:512], scalar1=cw_b[:, i, 0:1], scalar2=None,
                                                     op0=ALU.mult),
                    reads=["GT%d" % i, "cw_b"], writes=["cacc%d" % par])
                for k in range(1, ntv):
                    P.c("dve", (lambda k: lambda e: e.scalar_tensor_tensor(out=cacc[par], in0=GT[i][:, k:k + 512],
                                                                           scalar=cw_b[:, i, k:k + 1], in1=cacc[par],
                                                                           op0=ALU.mult, op1=ALU.add))(k),
                        reads=["GT%d" % i, "cw_b", "cacc%d" % par], writes=["cacc%d" % par])

            def stats(i):
                par = i % 2
                for (c0, n) in segs:
                    if n == 512:
                        pS, pQ, rS, rQ = bank(6), bank(7), "pb6", "pb7"
                    else:
                        b = rotP4.next()
                        pS, pQ, rS, rQ = bank(b)[:, 0:64], bank(b)[:, 64:128], "pb%d" % b, "pb%d" % b

                        def mms(e, pS=pS, pQ=pQ, c0=c0, n=n):
                            e.matmul(pS, lhsT=ones_b, rhs=cT[i][:, c0:c0 + n], start=True, stop=False)
                            return e.matmul(pQ, lhsT=ones_b, rhs=csq[par][:, c0:c0 + n], start=False, stop=True)
                        P.c("pe", mms, reads=["ones_b", "cT%d" % i, "csq%d" % par], writes=[rS])
                        if i == 0:
                            P.c("dve", (lambda b: lambda e: e.tensor_copy(out=sacc, in_=bank(b)[:, 0:128]))(b),
                                reads=[rS], writes=["sacc"])
                        else:
                            P.c("dve", (lambda b: lambda e: e.tensor_tensor(out=sacc, in0=sacc, in1=bank(b)[:, 0:128], op=ALU.add))(b),
                                reads=[rS, "sacc"], writes=["sacc"])
                        continue

                    def mm(e, pS=pS, pQ=pQ, c0=c0, n=n):
                        e.matmul(pS, lhsT=ones_b, rhs=cT[i][:, c0:c0 + n], start=(i == 0), stop=(i == 15))
                        return e.matmul(pQ, lhsT=ones_b, rhs=csq[par][:, c0:c0 + n], start=(i == 0), stop=(i == 15))
                    P.c("pe", mm, reads=["ones_b", "cT%d" % i, "csq%d" % par], writes=[rS, rQ])

            build_diag(0, 0)
            build_diag(0, 1)
            for i in range(16):
                wt, wres = wslot(blk, 16 + i // 2)
                sub = (i % 2) * 2
                glu_pre(i)
                for seg in segs:
                    a_outs = proj_fm(blk, wt, wres, sub, [seg], "a", rot=rotB)
                    gl_outs = proj_fm(blk, wt, wres, sub + 1, [seg], "gl", rot=rotB)
                    glu(i, a_outs, gl_outs)
                if i >= 1:
                    conv(i - 1, mid=(lambda i=i: stats(i - 2)) if i >= 2 else None)
                conv_taps_dve(i)
                if i % 2 == 1:
                    w_release(1)
            conv(15, mid=lambda: stats(14))
            stats(15)
            if debug:
                for i in range(16):
                    P.dma("sp", (lambda i: lambda e: e.dma_start(out=dbg_GT[blk, i, :, 0:542], in_=GT[i][:, 0:542]))(i), reads=["GT%d" % i])
                    P.dma("sp", (lambda i: lambda e: e.dma_start(out=dbg_cT[blk, i, :, 0:ncol], in_=cT[i][:, 0:ncol]))(i),
                          reads=["cT%d" % i])
                P.dma("sp", lambda e: e.dma_start(out=dbg_hT[2 * blk + 1, :, :, 0:ncol], in_=hT[:, :, 0:ncol]),
                      reads=["hT%d" % t[0] for t in tiles_of(blk)])
            for (c0, n) in segs:
                if n == 512:
                    pS, pQ, rS, rQ = bank(6), bank(7), "pb6", "pb7"
                else:
                    pS, pQ, rS, rQ = sacc[:, 0:64], sacc[:, 64:128], "sacc", "sacc"
                P.c("dve", (lambda pS, c0, n: lambda e: e.tensor_scalar(out=mu_bc[:, c0:c0 + n], in0=pS, scalar1=1.0 / E,
                                                                        scalar2=None, op0=ALU.mult))(pS, c0, n),
                    reads=[rS], writes=["mu"])
                P.c("dve", (lambda c0, n: lambda e: e.tensor_tensor(out=var_t[:, c0:c0 + n], in0=mu_bc[:, c0:c0 + n],
                                                                    in1=mu_bc[:, c0:c0 + n], op=ALU.mult))(c0, n),
                    reads=["mu"], writes=["var"])
                P.c("dve", (lambda pQ, c0, n: lambda e: e.scalar_tensor_tensor(out=var_t[:, c0:c0 + n], in0=pQ, scalar=1.0 / E,
                                                                               in1=var_t[:, c0:c0 + n], op0=ALU.mult,
                                                                               op1=ALU.subtract))(pQ, c0, n),
                    reads=[rQ, "var"], writes=["var"])
            P.c("act", lambda e: e.activation(out=var_t[:, 0:ncol], in_=var_t[:, 0:ncol], func=AF.Sqrt, bias=epsc, scale=1.0),
                reads=["var", "epsc"], writes=["var"])
            P.c("dve", lambda e: e.reciprocal(out=rstd_bc[:, 0:ncol], in_=var_t[:, 0:ncol]), reads=["var"], writes=["rstd"])

            if last:
                for grp in range(4):
                    b = rotP.next()

                    def trp(e, b=b, grp=grp):
                        ins = None
                        for q in range(4):
                            i = grp * 4 + q
                            ins = e.transpose(bank(b)[0:30, q * 128:(q + 1) * 128], g32[i][:, 0:30], ident_f)
                        return ins
                    P.c("pe", trp, reads=["g32_%d" % (grp * 4 + q) for q in range(4)] + ["ident_f"], writes=["pb%d" % b])
                    P.c("act", (lambda b, grp: lambda e: e.activation(out=cstage[0:30, grp * 512:(grp + 1) * 512],
                                                                      in_=bank(b)[0:30, :], func=AF.Copy))(b, grp),
                        reads=["pb%d" % b], writes=CST)
                P.dma("sp", lambda e: e.dma_start(out=cp_d, in_=cstage[0:30, :]), reads=CST)
                for grp in range(4):
                    b = rotP.next()

                    def trs(e, b=b, grp=grp):
                        ins = None
                        for q in range(4):
                            i = grp * 4 + q
                            ins = e.transpose(bank(b)[0:64, q * 128:(q + 1) * 128], g32s[i], ident_f)
                        return ins
                    P.c("pe", trs, reads=["g32_%d" % (grp * 4 + q) for q in range(4)] + ["ident_f"], writes=["pb%d" % b])
                    P.c("act", (lambda b, grp: lambda e: e.activation(out=cstage[0:64, grp * 512:(grp + 1) * 512],
                                                                      in_=bank(b)[0:64, :], func=AF.Copy))(b, grp),
                        reads=["pb%d" % b], writes=CST)
                for bb in range(16):
                    P.dma("sp", (lambda bb: lambda e: e.dma_start(out=cs_d[bb, 26:30, :],
                                                                  in_=cstage[4 * bb:4 * bb + 4, :]))(bb),
                          reads=CST)

            if blk + 1 < NBLK:
                load_lnbc(["dgA", "dgB", "GTs0", "GTs1", "stg0", "stg1", "sig0", "sig1"])
            def p2_front(i):
                wt, wres = wslot(blk, 24 + i // 4)
                par = i % 2
                z_outs = proj_fm(blk, wt, wres, i % 4, segs, "z", rot=rotB)
                for (pa, pr, c0, n) in z_outs:
                    P.c("act", (lambda pa, c0, n: lambda e: e.activation(out=szT[par][:, c0:c0 + n], in_=pa, func=AF.Silu))(pa, c0, n),
                        reads=[pr], writes=["szT%d" % par])
                P.c("dve", lambda e: e.tensor_tensor(out=cn[par][:, 0:ncol], in0=cT[i][:, 0:ncol], in1=mu_bc[:, 0:ncol], op=ALU.subtract),
                    reads=["cT%d" % i, "mu"], writes=["cn%d" % par])
                P.c("dve", lambda e: e.tensor_tensor(out=cn[par][:, 0:ncol], in0=cn[par][:, 0:ncol], in1=rstd_bc[:, 0:ncol], op=ALU.mult),
                    reads=["cn%d" % par, "rstd"], writes=["cn%d" % par])
                P.c("act", lambda e: e.activation(out=cn[par][:, 0:ncol], in_=cn[par][:, 0:ncol], func=AF.Silu,
                                                  scale=pB[:, 16 + i:17 + i], bias=pB[:, 32 + i:33 + i]),
                    reads=["cn%d" % par, "pB"], writes=["cn%d" % par])

            def p2_back(i):
                par = i % 2
                P.c("dve", lambda e: e.tensor_tensor(out=cT[i][:, 0:ncol], in0=cn[par][:, 0:ncol], in1=szT[par][:, 0:ncol], op=ALU.mult),
                    reads=["cn%d" % par, "szT%d" % par], writes=["cT%d" % i])
            for i in range(16):
                p2_front(i)
                if i >= 1:
                    p2_back(i - 1)
                if i % 4 == 3:
                    w_release(1)
            p2_back(15)
            if debug:
                for i in range(16):
                    P.dma("sp", (lambda i: lambda e: e.dma_start(out=dbg_oB[blk, i, :, 0:ncol], in_=cT[i][:, 0:ncol]))(i),
                          reads=["cT%d" % i])
            return o_phase(1, blk, cT, lambda i: "cT%d" % i, next_pro, defer_T=True)

        for tl in tiles_of(0):
            prologue(0, 0, tl)
        pend = []
        for blk in range(NBLK):
            fence()
            layer_a(blk, lambda tl, blk=blk: (1, blk, tl), pendT=pend)
            fence()
            if blk + 1 < NBLK:
                nxt = {t[0]: t for t in tiles_of(blk + 1)}

                def npro(tl, blk=blk, nxt=nxt):
                    if tl[0] in nxt:
                        return (0, blk + 1, nxt[tl[0]])
                    return None
                pend = layer_b(blk, npro)
                if blk + 1 == NBLK - 1:
                    prologue_dma(0, blk + 1, nxt[4])
                    pend.append((0, blk + 1, nxt[4]))
            else:
                pend = layer_b(blk, lambda tl: None)

        P.finalize(lambda n: es.enter_context(nc.semaphore(n)))
        with nc.Block() as block:
            P.emit(block)
    return nc


def _weight_stream(a_w_in, a_w_out, b_w_in, b_w_out):
    units = np.empty((UNITS_PER_BLOCK, 128, 4096), np.float32)

    def colblk(Wk, c0, n):
        return Wk[:, :, c0:c0 + n].transpose(1, 0, 2).reshape(128, 8 * n)

    Wk = a_w_in[0].reshape(8, 128, 3 * E)
    for j in range(4):
        units[j] = colblk(Wk, E + 512 * j, 512)
    for g in range(8):
        subs = [colblk(Wk, base + 128 * dt, 128) for base, dt in ((0, 2 * g), (0, 2 * g + 1), (2 * E, 2 * g), (2 * E, 2 * g + 1))]
        units[4 + g] = np.concatenate(subs, axis=1)
    Wo = a_w_out[0].reshape(16, 128, D)
    for j in range(4):
        units[12 + j] = Wo[4 * j:4 * j + 4].transpose(1, 0, 2).reshape(128, 4096)
    Wk = b_w_in[0].reshape(8, 128, 3 * E)
    for i2 in range(8):
        subs = []
        for e_ in (2 * i2, 2 * i2 + 1):
            for base in (0, E):
                subs.append(colblk(Wk, base + 128 * e_, 128))
        units[16 + i2] = np.concatenate(subs, axis=1)
    for j in range(4):
        subs = [colblk(Wk, 2 * E + 128 * e_, 128) for e_ in range(4 * j, 4 * j + 4)]
        units[24 + j] = np.concatenate(subs, axis=1)
    Wo = b_w_out[0].reshape(16, 128, D)
    for j in range(4):
        units[28 + j] = Wo[4 * j:4 * j + 4].transpose(1, 0, 2).reshape(128, 4096)
    return units


_NC_CACHE = {}
_DEBUG = False


def kernel(x_prompt, x_sample, state_conv, pre_norm_g, post_norm_g,
           a_w_in, a_ln_g, a_ln_b, a_w_s, a_b_s, a_w_out,
           b_w_in, b_conv_w, b_conv_b, b_ln_g, b_ln_b, b_w_out):
    f = lambda a: np.ascontiguousarray(np.asarray(a, dtype=np.float32))
    x_prompt, x_sample, state_conv = f(x_prompt), f(x_sample), f(state_conv)
    wst = _weight_stream(f(a_w_in), f(a_w_out), f(b_w_in), f(b_w_out))
    gvec = f(np.concatenate([f(pre_norm_g), f(post_norm_g)], axis=0))
    lnA = f(np.stack([f(a_ln_g)[0], f(a_ln_b)[0]], axis=0))
    ws = f(a_w_s)[0]
    wsT = f(ws.transpose(2, 0, 1).reshape(128, 8 * 128))
    w4 = ws[:, :4, :4].transpose(2, 0, 1)
    wsTs = f(np.tile(w4, (16, 1, 16)).reshape(64, 8 * 64))
    bs = f(a_b_s)[0]
    bsA = f(bs.reshape(1, 8 * 128))
    bsS = f(np.tile(bs[:, :4], (1, 16)).reshape(1, 8 * 64))
    cwT = f(f(b_conv_w)[0].T.reshape(16, 128, CW).transpose(1, 0, 2).reshape(128, 16 * CW))
    colp = lambda v: f(v)[0].reshape(16, 128).T
    pB = f(np.concatenate([colp(b_conv_b), colp(b_ln_g), colp(b_ln_b)], axis=1))

    in_maps = []
    for c in range(N_CORES):
        stc = state_conv[0, 16 * c:16 * (c + 1)]
        stT = f(stc.transpose(2, 0, 1).reshape(16, 128, 480).transpose(1, 0, 2))
        in_maps.append({
            "xp": f(x_prompt[c]), "xs": f(x_sample[16 * c:16 * (c + 1)].reshape(NS, D)),
            "stT": stT, "st": f(stc), "wst": wst, "gvec": gvec, "lnA": lnA, "wsT": wsT, "wsTs": wsTs,
            "bsA": bsA, "bsS": bsS, "cwT": cwT, "pB": pB,
        })
    if "nc" not in _NC_CACHE:
        _NC_CACHE["nc"] = build_nc(debug=_DEBUG)
    nc = _NC_CACHE["nc"]
    res = run_bass_kernel_spmd(nc, in_maps, core_ids=list(range(N_CORES)))
    r = res.results
    y_prompt = np.stack([r[c]["yp"] for c in range(N_CORES)], axis=0).astype(np.float32)
    y_sample = np.concatenate([r[c]["ys"].reshape(16, 4, D) for c in range(N_CORES)], axis=0).astype(np.float32)
    conv_p = np.stack([r[c]["cp"] for c in range(N_CORES)], axis=0)[None].astype(np.float32)
    conv_s = np.concatenate([r[c]["cs"] for c in range(N_CORES)], axis=0)[None].astype(np.float32)
    chunk_v = np.concatenate([r[c]["cv"].reshape(16, 4, E) for c in range(N_CORES)], axis=0)[None].astype(np.float32)
    return (y_prompt, y_sample, conv_p, conv_s, chunk_v)
```
